# Optimizing a Trainium2 kernel written in Bass

```python
import jax, jax.numpy as jnp
from jax import lax
import numpy as np

D_MODEL = 1024
BATCH = 8
SEQ = 2048
DEPTH = 2

N_MIXERS = 2
RMS_EPS = 1e-6
LN_EPS = 1e-5
CHUNK = 128
A_WIDTH = 2 * D_MODEL
A_GROUPS = 8
A_GROUP_DIM = A_WIDTH // A_GROUPS
B_WIDTH = 3 * D_MODEL // 2
B_HEADS = 12
B_HEAD_DIM = B_WIDTH // B_HEADS
CONV_WIDTH = 4
RG_C = 8.0

N_A_LAYERS = (DEPTH + 1) // 2
N_B_LAYERS = DEPTH // 2

kernel_name = "hybrid_sgu_rglru_trunk"


def rms_norm(x, w):
    xf = x.astype(jnp.float32)
    y = xf * lax.rsqrt(jnp.mean(xf * xf, axis=-1, keepdims=True) + RMS_EPS)
    return (y * w.astype(jnp.float32)).astype(x.dtype)


def layer_norm(x, w, b):
    xf = x.astype(jnp.float32)
    mu = jnp.mean(xf, axis=-1, keepdims=True)
    var = jnp.mean(jnp.square(xf - mu), axis=-1, keepdims=True)
    y = (xf - mu) * lax.rsqrt(var + LN_EPS)
    return (y * w.astype(jnp.float32) + b.astype(jnp.float32)).astype(x.dtype)


def chunked_sgu_mixer(h, w_in, ln_w, ln_b, w_s, b_s, w_out):
    B, S, _ = h.shape
    z = h @ w_in
    u, v, g = jnp.split(z, 3, axis=-1)
    u = jax.nn.gelu(u)
    v = layer_norm(jax.nn.gelu(v), ln_w, ln_b)
    v = v.reshape(B, S // CHUNK, CHUNK, A_GROUPS, A_GROUP_DIM)
    causal = jnp.tril(jnp.ones((CHUNK, CHUNK), dtype=w_s.dtype))
    w_causal = w_s * causal[None]
    s = jnp.einsum('gts,bnsgc->bntgc', w_causal, v) + b_s.T[None, None, :, :, None]
    y = u * s.reshape(B, S, A_WIDTH) * jax.nn.silu(g)
    return y @ w_out


def _linear_combine(left, right):
    a_l, b_l = left
    a_r, b_r = right
    return a_l * a_r, a_r * b_l + b_r


def rglru_mixer(h, w_in, conv_w, conv_b, gate_a_w, gate_a_b, gate_x_w, gate_x_b, lam, w_out):
    B, S, _ = h.shape
    xb, g = jnp.split(h @ w_in, 2, axis=-1)
    xp = jnp.pad(xb, ((0, 0), (CONV_WIDTH - 1, 0), (0, 0)))
    xc = conv_b + conv_w[CONV_WIDTH - 1] * xp[:, CONV_WIDTH - 1:CONV_WIDTH - 1 + S]
    for k in range(CONV_WIDTH - 1):
        xc = xc + conv_w[k] * xp[:, k:k + S]
    xh = xc.reshape(B, S, B_HEADS, B_HEAD_DIM)
    r = jax.nn.sigmoid(jnp.einsum('bshi,hij->bshj', xh, gate_a_w).reshape(B, S, B_WIDTH) + gate_a_b)
    i = jax.nn.sigmoid(jnp.einsum('bshi,hij->bshj', xh, gate_x_w).reshape(B, S, B_WIDTH) + gate_x_b)
    log_a = -RG_C * r.astype(jnp.float32) * jax.nn.softplus(-lam.astype(jnp.float32))
    a = jnp.exp(log_a)
    mult = jnp.sqrt(-jnp.expm1(2.0 * log_a))
    bterm = mult * (i * xc).astype(jnp.float32)
    _, hseq = lax.associative_scan(_linear_combine, (a, bterm), axis=1)
    y = hseq.astype(h.dtype) * jax.nn.silu(g)
    return y @ w_out


def setup_inputs(seed: int = 0) -> dict:
    key = jax.random.key(seed)
    ks = jax.random.split(key, 20)
    f32 = jnp.float32
    x = jax.random.normal(ks[0], (BATCH, SEQ, D_MODEL), f32)
    norm_w = 1.0 + 0.05 * jax.random.normal(ks[1], (DEPTH, D_MODEL), f32)
    a_w_in = jax.random.normal(ks[2], (N_A_LAYERS, D_MODEL, 3 * A_WIDTH), f32) * D_MODEL ** -0.5
    a_ln_w = 1.0 + 0.05 * jax.random.normal(ks[3], (N_A_LAYERS, A_WIDTH), f32)
    a_ln_b = 0.02 * jax.random.normal(ks[4], (N_A_LAYERS, A_WIDTH), f32)
    a_w_s = jax.random.normal(ks[5], (N_A_LAYERS, A_GROUPS, CHUNK, CHUNK), f32) * CHUNK ** -0.5
    a_b_s = 1.0 + 0.05 * jax.random.normal(ks[6], (N_A_LAYERS, A_GROUPS, CHUNK), f32)
    a_w_out = jax.random.normal(ks[7], (N_A_LAYERS, A_WIDTH, D_MODEL), f32) * A_WIDTH ** -0.5
    b_w_in = jax.random.normal(ks[8], (N_B_LAYERS, D_MODEL, 2 * B_WIDTH), f32) * D_MODEL ** -0.5
    b_conv_w = jax.random.normal(ks[9], (N_B_LAYERS, CONV_WIDTH, B_WIDTH), f32) * CONV_WIDTH ** -0.5
    b_conv_b = 0.02 * jax.random.normal(ks[10], (N_B_LAYERS, B_WIDTH), f32)
    b_gate_a_w = jax.random.normal(ks[11], (N_B_LAYERS, B_HEADS, B_HEAD_DIM, B_HEAD_DIM), f32) * B_HEAD_DIM ** -0.5
    b_gate_a_b = 0.02 * jax.random.normal(ks[12], (N_B_LAYERS, B_WIDTH), f32)
    b_gate_x_w = jax.random.normal(ks[13], (N_B_LAYERS, B_HEADS, B_HEAD_DIM, B_HEAD_DIM), f32) * B_HEAD_DIM ** -0.5
    b_gate_x_b = 0.02 * jax.random.normal(ks[14], (N_B_LAYERS, B_WIDTH), f32)
    a_c = jax.random.uniform(ks[15], (N_B_LAYERS, B_WIDTH), f32, minval=0.9, maxval=0.999)
    a0 = a_c ** (1.0 / RG_C)
    b_lambda = jnp.log(a0) - jnp.log1p(-a0)
    b_w_out = jax.random.normal(ks[16], (N_B_LAYERS, B_WIDTH, D_MODEL), f32) * B_WIDTH ** -0.5
    norm_f_w = 1.0 + 0.05 * jax.random.normal(ks[17], (D_MODEL,), f32)
    return {
        "x": x, "norm_w": norm_w,
        "a_w_in": a_w_in, "a_ln_w": a_ln_w, "a_ln_b": a_ln_b,
        "a_w_s": a_w_s, "a_b_s": a_b_s, "a_w_out": a_w_out,
        "b_w_in": b_w_in, "b_conv_w": b_conv_w, "b_conv_b": b_conv_b,
        "b_gate_a_w": b_gate_a_w, "b_gate_a_b": b_gate_a_b,
        "b_gate_x_w": b_gate_x_w, "b_gate_x_b": b_gate_x_b,
        "b_lambda": b_lambda, "b_w_out": b_w_out,
        "norm_f_w": norm_f_w,
    }


def reference(x, norm_w, a_w_in, a_ln_w, a_ln_b, a_w_s, a_b_s, a_w_out,
              b_w_in, b_conv_w, b_conv_b, b_gate_a_w, b_gate_a_b,
              b_gate_x_w, b_gate_x_b, b_lambda, b_w_out, norm_f_w):
    for layer in range(DEPTH):
        h = rms_norm(x, norm_w[layer])
        j = layer // N_MIXERS
        if layer % N_MIXERS == 0:
            y = chunked_sgu_mixer(h, a_w_in[j], a_ln_w[j], a_ln_b[j], a_w_s[j], a_b_s[j], a_w_out[j])
        else:
            y = rglru_mixer(h, b_w_in[j], b_conv_w[j], b_conv_b[j], b_gate_a_w[j], b_gate_a_b[j],
                            b_gate_x_w[j], b_gate_x_b[j], b_lambda[j], b_w_out[j])
        x = x + y
    return rms_norm(x, norm_f_w)
```

```python
import os
import numpy as np
import concourse.bass as bass
import concourse.mybir as mybir
from concourse.bass_utils import run_bass_kernel_spmd

F32 = mybir.dt.float32
BF16 = mybir.dt.bfloat16
AF = mybir.ActivationFunctionType
ALU = mybir.AluOpType

S = 2048
D = 1024
NT = 16
TB = 512
NTB = 4
AW = 2048
BW = 1536
RMS_EPS = 1e-6
LN_EPS = 1e-5

ENGS = ("pe", "act", "dve", "pool", "sp")
_EST = [None]


class Op:
    __slots__ = ("eng", "fn", "deps", "is_dma", "sem_key", "signal", "count", "idx", "n", "tbl",
                 "nbytes", "pos", "sdeps", "name")

    def __init__(self, eng, fn, is_dma=False, sem_key=None, n=512, tbl=None, nbytes=0):
        self.eng = eng
        self.fn = fn
        self.deps = set()
        self.is_dma = is_dma
        self.sem_key = sem_key
        self.signal = False
        self.count = 0
        self.idx = -1
        self.n = n
        self.tbl = tbl
        self.nbytes = nbytes
        self.pos = -1
        self.sdeps = ()
        self.name = None


class _Probe:
    def __init__(self):
        self.name = None
        self.out = None
        self.func = None

    def __getattr__(self, name):
        def call(*args, **kwargs):
            self.name = name
            self.out = kwargs.get("out", args[0] if args else None)
            self.func = kwargs.get("func")
            return None
        return call


_TBL = None


def _tbl_of(func):
    global _TBL
    if _TBL is None:
        _TBL = {AF.Tanh: ("exp", "gelu"), AF.Exp: ("exp",), AF.Gelu_apprx_tanh: ("gelu",),
                AF.Sqrt: ("sqrt",), AF.Ln: ("ln",)}
    return _TBL.get(func)


class Prog:
    def __init__(self, nc, same_engine_sync=True, reorder=True):
        self.nc = nc
        self.ops = []
        self.last_w = {}
        self.readers = {}
        self.same_engine_sync = same_engine_sync
        self.reorder = reorder
        self.final_dma_keys = []
        self.fence_op = None
        self.fence_start = 0
        self.skip_fence = False
        self.est_us = None

    def _add(self, op, reads, writes):
        op.idx = len(self.ops)
        if self.fence_op is not None and not self.skip_fence:
            op.deps.add(self.fence_op)
        excl = [k for k in reads if isinstance(k, tuple) and k[0] == "ps"]
        if excl:
            reads = [k for k in reads if k not in excl]
            writes = list(writes) + [k for k in excl if k not in writes]
        for k in reads:
            w = self.last_w.get(k)
            if w is not None:
                op.deps.add(w)
        for k in writes:
            w = self.last_w.get(k)
            if w is not None:
                op.deps.add(w)
            for r in self.readers.get(k, ()):
                op.deps.add(r)
        for k in reads:
            self.readers.setdefault(k, []).append(op.idx)
        for k in writes:
            self.last_w[k] = op.idx
            self.readers[k] = []
        op.deps.discard(op.idx)
        self.ops.append(op)
        return op

    @staticmethod
    def _probe(fn):
        pr = _Probe()
        try:
            fn(pr)
        except Exception:
            pass
        n, nbytes = 512, 1 << 20
        if pr.out is not None:
            shp = tuple(pr.out.shape)
            n = 1
            for d in shp[1:]:
                n *= int(d)
            nbytes = n * int(shp[0]) * (4 if pr.out.dtype == F32 else 2)
            if pr.name == "tensor_tensor_scan":
                n *= 2
        return n, nbytes, _tbl_of(pr.func), pr.name

    def op(self, eng, fn, reads=(), writes=()):
        n, _, tbl, name = self._probe(fn)
        o = Op(eng, fn, n=n, tbl=tbl)
        o.name = name
        return self._add(o, reads, writes)

    def dma(self, eng, fn, sem_key, reads=(), writes=(), final=False):
        _, nbytes, _, _ = self._probe(fn)
        o = self._add(Op(eng, fn, is_dma=True, sem_key=sem_key, nbytes=nbytes), reads, writes)
        if final:
            if sem_key not in self.final_dma_keys:
                self.final_dma_keys.append(sem_key)
        return o

    def fence(self, fn):
        o = Op("pool", fn, n=8)
        o.idx = len(self.ops)
        o.deps = set(range(self.fence_start, o.idx))
        self.ops.append(o)
        self.fence_op = o.idx
        self.fence_start = o.idx
        return o

    @staticmethod
    def _dur(o):
        n = o.n
        k = float(os.environ.get("KSCALE_" + o.eng.upper(), "1"))
        if k != 1.0 and not o.is_dma:
            o2 = Op(o.eng, None, n=o.n); o2.name = o.name
            return k * Prog._dur0(o2)
        return Prog._dur0(o)

    @staticmethod
    def _dur0(o):
        n = o.n
        if o.is_dma:
            return 1.2 if o.eng == "pool" else 0.1
        if o.eng == "pe":
            return 0.03 + n / 2400.0
        if o.eng == "act":
            return 0.25 + n / 1200.0
        if o.eng == "dve":
            return 0.16 + n / 960.0
        if o.eng == "pool":
            if o.name == "tensor_tensor":
                return 0.15 + n * 0.00175
            if o.name == "tensor_copy":
                return 0.2 + n * 0.0033
            return 0.18 + n * 0.0022
        return 0.1

    def _schedule(self):
        ops = self.ops
        per_eng = {e: [o.idx for o in ops if o.eng == e] for e in ENGS}
        if not self.reorder:
            for e in ENGS:
                for p, i in enumerate(per_eng[e]):
                    ops[i].pos = p
            return per_eng
        INF = float("inf")
        PRIO = int(os.environ.get("KPRIO", "1"))
        WINI = int(os.environ.get("KWIN", "0"))
        PW = float(os.environ.get("KPW", "6.0"))
        TWAIT = float(os.environ.get("KTWAIT", "0.5"))
        nops = len(ops)
        fin = [None] * nops
        succ = [[] for _ in range(nops)]
        left = [0] * nops
        for o in ops:
            left[o.idx] = len(o.deps)
            for d in o.deps:
                succ[d].append(o.idx)
        rdy = [0.0] * nops
        cp = [0.0] * nops
        if os.environ.get("KCP", "1") == "1":
            for i in range(nops - 1, -1, -1):
                c = 0.0
                for sidx in succ[i]:
                    if cp[sidx] > c:
                        c = cp[sidx]
                cp[i] = c + self._dur(ops[i]) + (2.0 if ops[i].is_dma else 0.0)
            if os.environ.get("KSEED"):
                import random
                rnd = random.Random(int(os.environ["KSEED"]))
                rr = float(os.environ.get("KNOISE", "0.05"))
                cp = [c * (1.0 + rr * (2 * rnd.random() - 1)) for c in cp]
        avail = {e: [] for e in ENGS}
        for o in ops:
            if left[o.idx] == 0:
                avail[o.eng].append(o.idx)
        t_e = {e: 0.0 for e in ENGS}
        npend = {e: len(per_eng[e]) for e in ENGS}
        order = {e: [] for e in ENGS}
        cur_tbl = [None]
        dma_free = [0.0]
        remaining = nops
        guard = 0
        while remaining:
            guard += 1
            assert guard < 4000000, "scheduler stuck"
            e = min((x for x in ENGS if npend[x]), key=lambda x: t_e[x])
            t = t_e[e]
            best = None
            best_key = None
            min_r = INF
            lim = (min(avail[e]) + WINI) if (avail[e] and WINI) else None
            for i in avail[e]:
                if lim is not None and i > lim:
                    continue
                r = rdy[i]
                if r < min_r:
                    min_r = r
                if r <= t:
                    o = ops[i]
                    pen = 0
                    if e == "act" and o.tbl is not None and cur_tbl[0] is not None and cur_tbl[0] not in o.tbl:
                        pen = 1
                    key = (pen, -cp[i], i) if PRIO == 0 else ((-cp[i] + PW * pen, i) if PRIO == 1 else (0, -cp[i], i))
                    if best_key is None or key < best_key:
                        best_key = key
                        best = i
            if best is None:
                if min_r < INF:
                    t_e[e] = max(t, min_r)
                else:
                    others = [t_e[x] for x in ENGS if x != e and npend[x] and t_e[x] > t]
                    if others:
                        t_e[e] = min(others) + 1e-3
                    else:
                        others = [t_e[x] for x in ENGS if x != e and npend[x]]
                        assert others, "dependency deadlock in scheduler"
                        t_e[e] = max(others) + 1e-3
                continue
            o = ops[best]
            if (e == "act" and TWAIT > 0 and o.tbl is not None and cur_tbl[0] is not None
                    and cur_tbl[0] not in o.tbl):
                soon = [rdy[i] for i in avail[e]
                        if ops[i].tbl is not None and cur_tbl[0] in ops[i].tbl and t < rdy[i] <= t + TWAIT]
                if soon:
                    t_e[e] = min(soon)
                    continue
            start = t
            if e == "act" and o.tbl is not None:
                if cur_tbl[0] is None or cur_tbl[0] not in o.tbl:
                    start += float(os.environ.get("KTBL", "1.3"))
                    self.n_tbl = getattr(self, "n_tbl", 0) + 1
                    cur_tbl[0] = o.tbl[0]
            d = self._dur(o)
            t_e[e] = start + d
            if o.is_dma:
                s0 = max(start + d, dma_free[0])
                dma_free[0] = s0 + o.nbytes / 120e3
                fin[best] = dma_free[0] + 2.0
            else:
                fin[best] = start + d
            o.pos = len(order[e])
            order[e].append(best)
            avail[e].remove(best)
            npend[e] -= 1
            remaining -= 1
            fb = fin[best]
            for sidx in succ[best]:
                so = ops[sidx]
                lat = 0.0 if (so.eng == e and not o.is_dma) else float(os.environ.get("KLAT", "0.25"))
                if fb + lat > rdy[sidx]:
                    rdy[sidx] = fb + lat
                left[sidx] -= 1
                if left[sidx] == 0:
                    avail[so.eng].append(sidx)
        self.est_us = max(f for f in fin if f is not None)
        self.fin = fin
        return order

    def emit(self):
        nc = self.nc
        ops = self.ops
        order = self._schedule()
        for o in ops:
            if o.is_dma:
                o.signal = True
            best = {}
            for d in o.deps:
                p = ops[d]
                if p.is_dma:
                    k = ("dma", p.sem_key)
                else:
                    if p.eng == o.eng and not o.is_dma:
                        if o.eng == "pe" or not self.same_engine_sync:
                            continue
                    k = ("eng", p.eng)
                q = best.get(k)
                if q is None or ops[q].pos < p.pos:
                    best[k] = d
            o.sdeps = tuple(best.values())
            for d in o.sdeps:
                ops[d].signal = True
        eng_cnt = {e: 0 for e in ENGS}
        dma_cnt = {}
        for e in ENGS:
            for i in order[e]:
                o = ops[i]
                if not o.signal:
                    continue
                if o.is_dma:
                    dma_cnt[o.sem_key] = dma_cnt.get(o.sem_key, 0) + 16
                    o.count = dma_cnt[o.sem_key]
                else:
                    eng_cnt[e] += 1
                    o.count = eng_cnt[e]
        sems = {}
        for e in ENGS:
            if eng_cnt[e] > 0:
                sems[("eng", e)] = nc.alloc_semaphore("s_" + e)
        for i, k in enumerate(dma_cnt):
            sems[("dma", k)] = nc.alloc_semaphore("d_%d" % i)
        self.n_sems = len(sems)

        def sem_of(p):
            return sems[("dma", p.sem_key)] if p.is_dma else sems[("eng", p.eng)]

        final_waits = [(sems[("dma", k)], dma_cnt[k]) for k in self.final_dma_keys]

        def run(eng_name, eng):
            waited = {}
            for i in order[eng_name]:
                o = ops[i]
                for d in o.sdeps:
                    p = ops[d]
                    s = sem_of(p)
                    if waited.get(id(s), 0) < p.count:
                        eng.wait_ge(s, p.count)
                        waited[id(s)] = p.count
                ins = o.fn(eng)
                if o.signal:
                    ins.then_inc(sem_of(o), 16 if o.is_dma else 1)
            if eng_name == "sp":
                for s, c in final_waits:
                    eng.wait_ge(s, c)

        with nc.Block() as block:
            @block.sync
            def _(e):
                run("sp", e)

            @block.tensor
            def _(e):
                run("pe", e)

            @block.scalar
            def _(e):
                run("act", e)

            @block.vector
            def _(e):
                run("dve", e)

            @block.gpsimd
            def _(e):
                run("pool", e)


class _StopBuild(Exception):
    pass


class Ring:
    def __init__(self, name, tiles, keys=None):
        self.name = name
        self.tiles = tiles
        self.keys = keys if keys is not None else [(name, k) for k in range(len(tiles))]
        self.i = 0

    def next(self):
        k = self.i % len(self.tiles)
        self.i += 1
        return self.tiles[k], self.keys[k]


def build_program(layers="AB", final_norm=True):
    nc = bass.Bass("TRN2", target_bir_lowering=False)
    P = Prog(nc)
    doA = "A" in layers
    stop = os.environ.get("KSTOP", "")

    def chk(tag):
        if stop == tag:
            raise _StopBuild()
    doB = "B" in layers

    def din(name, shape):
        return nc.dram_tensor(name, list(shape), F32, kind="ExternalInput").ap()

    x_d = din("x", [S, D])
    norm_w_d = din("norm_w", [2, D])
    norm_f_w_d = din("norm_f_w", [D])
    if doA:
        a_w_in_d = din("a_w_in", [D, 3 * AW])
        a_ln_w_d = din("a_ln_w", [AW])
        a_ln_b_d = din("a_ln_b", [AW])
        a_w_s_d = din("a_w_s", [8, 128, 128])
        a_b_s_d = din("a_b_s", [8, 128])
        a_w_out_d = din("a_w_out", [AW, D])
    if doB:
        b_w_in_d = din("b_w_in", [D, 2 * BW])
        b_conv_w_d = din("b_conv_w", [4, BW])
        b_conv_b_d = din("b_conv_b", [BW])
        b_ga_w_d = din("b_gate_a_w", [12, 128, 128])
        b_ga_b_d = din("b_gate_a_b", [BW])
        b_gx_w_d = din("b_gate_x_w", [12, 128, 128])
        b_gx_b_d = din("b_gate_x_b", [BW])
        b_lam_d = din("b_lambda", [BW])
        b_w_out_d = din("b_w_out", [BW, D])
    out_d = nc.dram_tensor("out", [S, D], F32, kind="ExternalOutput").ap()

    base = (nc.sbuf_base + 63) // 64 * 64
    top = nc.sbuf_top
    cur = [base]

    def alloc(name, shape, dt, at=None):
        nbytes = int(np.prod(shape[1:])) * (4 if dt == F32 else 2)
        nbytes = (nbytes + 63) // 64 * 64
        if at is None:
            off = cur[0]
            cur[0] += nbytes
        else:
            off = at
        if off + nbytes > top and os.environ.get("KNOMEM"):
            off = base
        assert off + nbytes <= top, (name, off, nbytes, top)
        return nc.alloc_sbuf_tensor_at(name, list(shape), dt, offset=off), off, nbytes

    x_sb, _, _ = alloc("x_sb", [128, NT, D], F32)
    ident, _, _ = alloc("ident", [128, 128], F32)
    NHT = int(os.environ.get("KNHT", "1"))
    hTs = []
    for _i in range(NHT):
        _t, _o, _ = alloc("hT%d" % _i, [128, 8, TB], BF16)
        hTs.append(_t)
        if _i == 0:
            hT_off = _o
    CUR = {"hT": hTs[0], "hk": [("hT", 0, j) for j in range(4)]}
    xs = [alloc("xs%d" % i, [128, D], F32)[0] for i in range(2)]
    nw_pp, _, _ = alloc("nw_pp", [128, 16], F32)
    ss_t, _, _ = alloc("ss_t", [128, NT], F32)
    ms_t, _, _ = alloc("ms_t", [128, NT], F32)
    rstd_t, _, _ = alloc("rstd_t", [128, NT], F32)
    neg_half, _, _ = alloc("neg_half", [128, 1], F32)
    REG0 = cur[0]

    psum = [nc.alloc_psum_tensor("ps%d" % i, [128, 512], F32) for i in range(8)]
    NPS = int(os.environ.get("KNPS", "8"))
    PS = Ring("ps", [psum[i % 8] for i in range(NPS)])
    PSC = Ring("ps", psum[0:6], keys=[("ps", k) for k in range(6)])
    PSP = Ring("ps", psum[6:8], keys=[("ps", k) for k in (6, 7)])
    PSB = {"blk": PS, "chk": PS}
    XS = Ring("xs", xs)

    def ncdma(fn):
        def f(e):
            with nc.allow_non_contiguous_dma(reason="small parameter relayout"):
                return fn(e)
        return f

    for tb in range(NTB):
        P.dma("sp", lambda e, tb=tb: e.dma_start(
            out=x_sb[:, tb * 4:(tb + 1) * 4, :],
            in_=x_d[tb * TB:(tb + 1) * TB, :].rearrange("(j p) d -> p j d", p=128)),
            ("xl", tb), writes=[("x", tb * 4 + j) for j in range(4)] + ["xchain"])
    P.dma("sp", ncdma(lambda e: e.dma_start(out=nw_pp[:, :].rearrange("p (l k) -> p l k", l=2),
                                            in_=norm_w_d.rearrange("l (k p) -> p l k", p=128))),
          "nw", writes=["nw"])
    P.op("pool", lambda e: e.memset(ident[:, :], 0.0), writes=["ident"])
    P.op("pool", lambda e: e.affine_select(out=ident[:, :], in_=ident[:, :], pattern=[[-1, 128]],
                                           compare_op=ALU.not_equal, fill=1.0, base=0,
                                           channel_multiplier=1),
         reads=["ident"], writes=["ident"])
    P.op("pool", lambda e: e.memset(neg_half[:, :], -0.5), writes=["neg_half"])

    evac_flip = [0]

    def rms_stats(jt, xkey):
        junk, jk = XS.next()
        P.op("act", lambda e: e.activation(out=junk[:, :], in_=x_sb[:, jt, :], func=AF.Square,
                                           accum_out=ss_t[:, jt:jt + 1]),
             reads=[xkey], writes=[jk, ("ss", jt)])
        P.op("dve", lambda e: e.tensor_scalar(out=ms_t[:, jt:jt + 1], in0=ss_t[:, jt:jt + 1],
                                              scalar1=1.0 / D, scalar2=RMS_EPS,
                                              op0=ALU.mult, op1=ALU.add),
             reads=[("ss", jt)], writes=[("ms", jt)])
        P.op("pool", lambda e: e.tensor_tensor(out=rstd_t[:, jt:jt + 1], in0=ms_t[:, jt:jt + 1],
                                               in1=neg_half[:, :], op=ALU.pow),
             reads=[("ms", jt), "neg_half"], writes=[("rstd", jt)])

    def phase0(layer, tb):
        hb = tb % NHT
        CUR["hT"] = hTs[hb]
        CUR["hk"] = [("hT", hb, j) for j in range(4)]
        hT = hTs[hb]
        for j in range(4):
            jt = tb * 4 + j
            xkey = ("x", jt)
            rms_stats(jt, xkey)
            xt, xk = XS.next()
            P.op("act", lambda e, xt=xt, jt=jt: e.activation(out=xt[:, :], in_=x_sb[:, jt, :],
                                                             func=AF.Copy,
                                                             scale=rstd_t[:, jt:jt + 1]),
                 reads=[xkey, ("rstd", jt)], writes=[xk])
            for half in range(2):
                bank, bk = PSB["blk"].next()
                for q in range(4):
                    dk = half * 4 + q
                    P.op("pe", lambda e, bank=bank, q=q, dk=dk, xt=xt: e.transpose(
                        out=bank[:, q * 128:(q + 1) * 128], in_=xt[:, dk * 128:(dk + 1) * 128],
                        identity=ident[:, :]),
                        reads=[xk, "ident"], writes=[bk])
                for q in range(4):
                    dk = half * 4 + q
                    col = layer * 8 + dk
                    dst = hT[:, dk, j * 128:(j + 1) * 128]
                    src = bank[:, q * 128:(q + 1) * 128]
                    _ev = os.environ.get("KEVAC%d" % layer, "dve" if layer == 0 else "alt")
                    if _ev == "dve" or (_ev == "alt" and (evac_flip[0] // 4) % 2 == 0):
                        P.op("dve", lambda e, dst=dst, src=src, col=col: e.tensor_scalar(
                            out=dst, in0=src, scalar1=nw_pp[:, col:col + 1], scalar2=None,
                            op0=ALU.mult),
                            reads=[bk, "nw"], writes=[("hT", hb, j)])
                    else:
                        P.op("act", lambda e, dst=dst, src=src, col=col: e.activation(
                            out=dst, in_=src, func=AF.Copy, scale=nw_pp[:, col:col + 1]),
                            reads=[bk, "nw"], writes=[("hT", hb, j)])
                    evac_flip[0] += 1


    def residual_add(bank, bk, jt, dh):
        P.op("dve", lambda e: e.tensor_tensor(out=x_sb[:, jt, dh * 512:(dh + 1) * 512],
                                              in0=x_sb[:, jt, dh * 512:(dh + 1) * 512],
                                              in1=bank[:, :], op=ALU.add),
             reads=[bk, ("x", jt)], writes=[("x", jt)])

    NFW = [None]
    done_tiles = set()
    B_EARLY_KEYS = (["cw", "cb", "ba", "bx", "k1", "k2", "quarter", "nfw", "waB", "wxB"]
                    + [("halo", c) for c in range(12)] + [("hstate", c) for c in range(12)]
                    + [("winB", p, h) for p in range(2) for h in range(2)]
                    + [("bch", g, i) for g in range(3) for i in range(2)]
                    + [(r, k) for r in ("HD", "XC", "XCB", "THR", "A", "THI", "THG") for k in range(4)])

    def finalize(jt):
        done_tiles.add(jt)
        if final_norm:
            nfw_b = NFW[0]
            rms_stats(jt, ("x", jt))
            P.op("dve", lambda e: e.scalar_tensor_tensor(
                out=x_sb[:, jt, :], in0=x_sb[:, jt, :], scalar=rstd_t[:, jt:jt + 1], in1=nfw_b[:, :],
                op0=ALU.mult, op1=ALU.mult),
                reads=[("x", jt), ("rstd", jt), "nfw"], writes=[("x", jt)])
            P.dma("sp", lambda e: e.dma_start(out=out_d[jt * 128:(jt + 1) * 128, :],
                                              in_=x_sb[:, jt, :]),
                  ("out", jt % 4), reads=[("x", jt)], final=True)
        else:
            P.dma("sp", lambda e: e.dma_start(out=out_d[jt * 128:(jt + 1) * 128, :],
                                              in_=x_sb[:, jt, :]),
                  ("out", jt % 4), reads=[("x", jt)], final=True)

    try:
        if doA:
            cur[0] = REG0
            lnw_pp, _, _ = alloc("lnw_pp", [128, 16], F32)
            stats, _, _ = alloc("stats", [128, 4, 24], F32)
            mv, _, _ = alloc("mv", [128, 4, 2], F32)
            lve, _, _ = alloc("lve", [128, 4], F32)
            lrstd, _, _ = alloc("lrstd", [128, 4], F32)
            nmr, _, _ = alloc("nmr", [128, 4], F32)
            stages = [alloc("stg%d" % i, [128, 8, 512], BF16)[0] for i in range(4)]
            biasT, _, _ = alloc("biasT", [128, 16, 128], F32)
            wcT, _, _ = alloc("wcT", [128, 8, 128], BF16)
            _gu0 = cur[0]
            GUr = Ring("GU", [alloc("GU%d" % i, [128, 512], F32)[0] for i in range(2)])
            THr = Ring("TH", [alloc("TH%d" % i, [128, 512], F32)[0] for i in range(2)])
            SSr = Ring("SS", [alloc("SS%d" % i, [128, 512], F32)[0] for i in range(2)])
            vn, vn_off, _ = alloc("vn", [128, 4, AW], BF16)
            A_EARLY_END = cur[0]
            woutA, _, _ = alloc("woutA", [128, 16, D], BF16)
            yT, _, _ = alloc("yT", [128, 16, TB], BF16)
            so = [_gu0]

            def salloc(name, shape, dt):
                t, off, nb = alloc(name, shape, dt, at=so[0])
                so[0] += nb
                return t
            ws_raw = salloc("ws_raw", [128, 8, 128], F32)
            wcT_f = salloc("wcT_f", [128, 8, 128], F32)
            lnb_row = salloc("lnb_row", [1, AW], F32)
            r_row = salloc("r_row", [1, 1024], F32)
            bs_row = salloc("bs_row", [1, 1024], F32)
            ones_row = salloc("ones_row", [1, 128], F32)
            ones_col = salloc("ones_col", [128, 1], F32)
            assert so[0] <= A_EARLY_END
            if os.environ.get("KMEM"):
                print("A mem end", cur[0], "top", top, "spare", top - cur[0])

            P.dma("sp", ncdma(lambda e: e.dma_start(out=lnw_pp[:, :],
                                                    in_=a_ln_w_d.rearrange("(k p) -> p k", p=128))),
                  "lnw", writes=["lnw"])
            P.dma("sp", lambda e: e.dma_start(out=ws_raw[:, :, :],
                                              in_=a_w_s_d.rearrange("g t s -> t g s")),
                  "ws", writes=["ws_raw"])
            P.dma("sp", lambda e: e.dma_start(out=lnb_row[:, :],
                                              in_=a_ln_b_d.rearrange("(o n) -> o n", o=1)),
                  "lnb", writes=["lnb_row"])
            P.dma("sp", lambda e: e.dma_start(out=bs_row[:, :],
                                              in_=a_b_s_d.rearrange("(o g) t -> o (g t)", o=1)),
                  "bs", writes=["bs_row"])
            P.op("pool", lambda e: e.memset(ones_row[:, :], 1.0), writes=["ones_row"])
            P.op("pool", lambda e: e.memset(ones_col[:, :], 1.0), writes=["ones_col"])
            P.op("pool", lambda e: e.affine_select(out=ws_raw[:, :, :], in_=ws_raw[:, :, :],
                                                   pattern=[[0, 8], [-1, 128]], compare_op=ALU.is_ge,
                                                   fill=0.0, base=0, channel_multiplier=1),
                 reads=["ws_raw"], writes=["ws_raw"])
            for half in range(2):
                bank, bk = PS.next()
                for q in range(4):
                    g = half * 4 + q
                    P.op("pe", lambda e, bank=bank, q=q, g=g: e.transpose(
                        out=bank[:, q * 128:(q + 1) * 128], in_=ws_raw[:, g, :], identity=ident[:, :]),
                        reads=["ws_raw", "ident"], writes=[bk])
                P.op("dve", lambda e, bank=bank, half=half: e.tensor_copy(
                    out=wcT_f[:, half * 4:(half + 1) * 4, :].rearrange("p g t -> p (g t)"),
                    in_=bank[:, :]), reads=[bk], writes=["wcT_f"])
            P.op("dve", lambda e: e.tensor_copy(out=wcT[:, :, :], in_=wcT_f[:, :, :]),
                 reads=["wcT_f"], writes=["wcT"])
            for half in range(2):
                bank, bk = PS.next()
                P.op("pe", lambda e, bank=bank, half=half: e.matmul(
                    bank[0:1, :], lhsT=ones_col[:, :],
                    rhs=wcT_f[:, half * 4:(half + 1) * 4, :].rearrange("p g t -> p (g t)"),
                    start=True, stop=True),
                    reads=["ones_col", "wcT_f"], writes=[bk])
                P.op("dve", lambda e, bank=bank, half=half: e.tensor_copy(
                    out=r_row[:, half * 512:(half + 1) * 512], in_=bank[0:1, :]),
                    reads=[bk], writes=["r_row"])
            for q4 in range(4):
                bank, bk = PS.next()
                for q in range(4):
                    cc = q4 * 4 + q
                    g = cc // 2
                    P.op("pe", lambda e, bank=bank, q=q, cc=cc, g=g: e.matmul(
                        bank[:, q * 128:(q + 1) * 128], lhsT=lnb_row[:, cc * 128:(cc + 1) * 128],
                        rhs=r_row[:, g * 128:(g + 1) * 128], start=True, stop=False),
                        reads=["lnb_row", "r_row"], writes=[bk])
                    P.op("pe", lambda e, bank=bank, q=q, cc=cc, g=g: e.matmul(
                        bank[:, q * 128:(q + 1) * 128], lhsT=ones_row[:, :],
                        rhs=bs_row[:, g * 128:(g + 1) * 128], start=False, stop=True),
                        reads=["ones_row", "bs_row"], writes=[bk])
                P.op("dve", lambda e, bank=bank, q4=q4: e.tensor_copy(
                    out=biasT[:, q4 * 4:(q4 + 1) * 4, :].rearrange("p c t -> p (c t)"),
                    in_=bank[:, :]), reads=[bk], writes=["biasT"])
            alias_keys = ["ws_raw", "wcT_f", "lnb_row", "r_row", "bs_row", "ones_row", "ones_col"]
            P.op("pool", lambda e: e.memset(ones_col[:, :], 1.0),
                 writes=alias_keys + [("vn", j, nb) for j in range(4) for nb in range(4)]
                 + [(r, k) for r in ("GU", "TH", "SS") for k in range(2)])
            chk("setup")

            stream = []
            for tb in range(NTB):
                for nb in range(4):
                    stream.append(AW + nb * 512)
                for cg in range(4):
                    stream.append(cg * 512)
                    stream.append(2 * AW + cg * 512)
            st_issued = [0]

            def issue_loads(upto):
                while st_issued[0] < min(upto, len(stream)):
                    i = st_issued[0]
                    col = stream[i]
                    s = i % 4
                    P.dma("pool", lambda e, s=s, col=col: e.dma_start(
                        out=stages[s][:, :, :],
                        in_=a_w_in_d[:, col:col + 512].rearrange("(k p) n -> p k n", p=128)),
                        ("stg", s), writes=[("stg", s)] + (["pchain"] if i < 3 else []))
                    st_issued[0] += 1
            st_used = [0]

            def next_stage():
                i = st_used[0]
                issue_loads(i + 1)
                st_used[0] += 1
                return stages[i % 4], ("stg", i % 4)

            issue_loads(3)
            for q4 in range(4):
                P.dma("pool", lambda e, q4=q4: e.dma_start(
                    out=woutA[:, q4 * 4:(q4 + 1) * 4, :],
                    in_=a_w_out_d[q4 * 512:(q4 + 1) * 512, :].rearrange("(c p) d -> p c d", p=128)),
                    ("woutA", q4), writes=[("woutA", q4), "pchain"])
            for tb in range(NTB):
                phase0(0, tb)
                chk("p0")
                for nb in range(4):
                    stg, sk = next_stage()
                    for j in range(4):
                        bank, bk = PS.next()
                        for dk in range(8):
                            P.op("pe", lambda e, bank=bank, stg=stg, dk=dk, j=j, hT=CUR["hT"]: e.matmul(
                                bank[:, :], lhsT=hT[:, dk, j * 128:(j + 1) * 128], rhs=stg[:, dk, :],
                                start=(dk == 0), stop=(dk == 7)),
                                reads=[CUR["hk"][j], sk], writes=[bk])
                        P.op("act", lambda e, bank=bank, j=j, nb=nb: e.activation(
                            out=vn[:, j, nb * 512:(nb + 1) * 512], in_=bank[:, :],
                            func=AF.Gelu_apprx_tanh),
                            reads=[bk], writes=[("vn", j, nb)])
                        P.op("dve", lambda e, j=j, nb=nb: e.bn_stats(
                            out=stats[:, j, nb * 6:(nb + 1) * 6], in_=vn[:, j, nb * 512:(nb + 1) * 512]),
                            reads=[("vn", j, nb)], writes=[("stats", j, nb)])
                    issue_loads(st_used[0] + 3)
                for j in range(4):
                    P.op("dve", lambda e, j=j: e.bn_aggr(out=mv[:, j, :], in_=stats[:, j, :]),
                         reads=[("stats", j, nb) for nb in range(4)], writes=[("mv", j)])
                    P.op("dve", lambda e, j=j: e.tensor_scalar(
                        out=lve[:, j:j + 1], in0=mv[:, j, 1:2], scalar1=LN_EPS, scalar2=None,
                        op0=ALU.add), reads=[("mv", j)], writes=[("lve", j)])
                    P.op("pool", lambda e, j=j: e.tensor_tensor(
                        out=lrstd[:, j:j + 1], in0=lve[:, j:j + 1], in1=neg_half[:, :], op=ALU.pow),
                        reads=[("lve", j), "neg_half"], writes=[("lrstd", j)])
                    P.op("dve", lambda e, j=j: e.scalar_tensor_tensor(
                        out=nmr[:, j:j + 1], in0=mv[:, j, 0:1], scalar=-1.0, in1=lrstd[:, j:j + 1],
                        op0=ALU.mult, op1=ALU.mult),
                        reads=[("mv", j), ("lrstd", j)], writes=[("nmr", j)])
                    P.op("act", lambda e, j=j: e.activation(
                        out=vn[:, j, :], in_=vn[:, j, :], func=AF.Identity,
                        scale=lrstd[:, j:j + 1], bias=nmr[:, j:j + 1]),
                        reads=[("vn", j, nb) for nb in range(4)] + [("lrstd", j), ("nmr", j)],
                        writes=[("vn", j, nb) for nb in range(4)])
                chk("p1")
                for cg in range(4):
                    stU, skU = next_stage()
                    stG, skG = next_stage()
                    for ci in range(4):
                        cc = cg * 4 + ci
                        g = cc // 2
                        bU, bkU = PS.next()
                        for dk in range(8):
                            P.op("pe", lambda e, bU=bU, stU=stU, dk=dk, ci=ci, hT=CUR["hT"]: e.matmul(
                                bU[:, :], lhsT=stU[:, dk, ci * 128:(ci + 1) * 128], rhs=hT[:, dk, :],
                                start=(dk == 0), stop=(dk == 7)),
                                reads=CUR["hk"] + [skU], writes=[bkU])
                        bG, bkG = PS.next()
                        for dk in range(8):
                            P.op("pe", lambda e, bG=bG, stG=stG, dk=dk, ci=ci, hT=CUR["hT"]: e.matmul(
                                bG[:, :], lhsT=stG[:, dk, ci * 128:(ci + 1) * 128], rhs=hT[:, dk, :],
                                start=(dk == 0), stop=(dk == 7)),
                                reads=CUR["hk"] + [skG], writes=[bkG])
                        bS, bkS = PS.next()
                        for j in range(4):
                            P.op("pe", lambda e, bS=bS, j=j, cc=cc, g=g: e.matmul(
                                bS[:, j * 128:(j + 1) * 128], lhsT=vn[:, j, cc * 128:(cc + 1) * 128],
                                rhs=wcT[:, g, :], start=True, stop=True),
                                reads=[("vn", j, cc // 4), "wcT"], writes=[bkS])
                        GU, gk = GUr.next()
                        TH, tk = THr.next()
                        SSt, sk2 = SSr.next()
                        P.op("act", lambda e, GU=GU, bU=bU: e.activation(
                            out=GU[:, :], in_=bU[:, :], func=AF.Gelu_apprx_tanh),
                            reads=[bkU], writes=[gk])
                        P.op("act", lambda e, TH=TH, bG=bG: e.activation(
                            out=TH[:, :], in_=bG[:, :], func=AF.Tanh, scale=0.5),
                            reads=[bkG], writes=[tk])
                        P.op("dve", lambda e, SSt=SSt, bS=bS, cc=cc: e.scalar_tensor_tensor(
                            out=SSt[:, :].rearrange("p (j t) -> p j t", j=4),
                            in0=bS[:, :].rearrange("p (j t) -> p j t", j=4),
                            scalar=lnw_pp[:, cc:cc + 1],
                            in1=biasT[:, cc, :].unsqueeze(1).broadcast_to([128, 4, 128]),
                            op0=ALU.mult, op1=ALU.add),
                            reads=[bkS, "lnw", "biasT"], writes=[sk2])
                        P.op("dve", lambda e, TH=TH, bG=bG: e.scalar_tensor_tensor(
                            out=TH[:, :], in0=TH[:, :], scalar=1.0, in1=bG[:, :],
                            op0=ALU.add, op1=ALU.mult),
                            reads=[tk, bkG], writes=[tk])
                        P.op("dve", lambda e, GU=GU, TH=TH: e.scalar_tensor_tensor(
                            out=GU[:, :], in0=GU[:, :], scalar=0.5, in1=TH[:, :],
                            op0=ALU.mult, op1=ALU.mult),
                            reads=[gk, tk], writes=[gk])
                        P.op("dve", lambda e, GU=GU, SSt=SSt, cc=cc: e.tensor_tensor(
                            out=yT[:, cc, :], in0=SSt[:, :], in1=GU[:, :], op=ALU.mult),
                            reads=[gk, sk2], writes=[("yT", cc)])
                    issue_loads(st_used[0] + 3)
                chk("p2")
                if tb == NTB - 1 and doB:
                    a_early = (["lnw", "biasT", "wcT"] + [("stg", i) for i in range(4)]
                               + [("stats", j, nb) for j in range(4) for nb in range(4)]
                               + [(k, j) for k in ("mv", "lve", "lrstd", "nmr") for j in range(4)]
                               + [(r, k) for r in ("GU", "TH", "SS") for k in range(2)]
                               + [("vn", j, nb) for j in range(4) for nb in range(4)])
                    P.op("pool", lambda e: e.memset(ones_col[:, :], 1.0), writes=a_early + B_EARLY_KEYS)
                for j in range(4):
                    jt = tb * 4 + j
                    for dh in range(2):
                        bank, bk = PS.next()
                        for cc in range(16):
                            P.op("pe", lambda e, bank=bank, cc=cc, j=j, dh=dh: e.matmul(
                                bank[:, :], lhsT=yT[:, cc, j * 128:(j + 1) * 128],
                                rhs=woutA[:, cc, dh * 512:(dh + 1) * 512],
                                start=(cc == 0), stop=(cc == 15)),
                                reads=[("yT", cc), ("woutA", cc // 4)], writes=[bk])
                        residual_add(bank, bk, jt, dh)
                chk("p3")

        if doB:
            if doA:
                P.fence(lambda e: e.memset(neg_half[:, :], -0.5))
            cur[0] = REG0
            cw_pp, _, _ = alloc("cw_pp", [128, 48], F32)
            cb_pp, _, _ = alloc("cb_pp", [128, 12], F32)
            ba_pp, _, _ = alloc("ba_pp", [128, 12], F32)
            bx_pp, _, _ = alloc("bx_pp", [128, 12], F32)
            k1_pp, _, _ = alloc("k1_pp", [128, 12], F32)
            k2_pp, _, _ = alloc("k2_pp", [128, 12], F32)
            halo, _, _ = alloc("halo", [128, 12, 4], F32)
            hstate, _, _ = alloc("hstate", [128, 12], F32)
            quarter, _, _ = alloc("quarter", [128, 1], F32)
            if final_norm:
                NFW[0] = alloc("nfw_b", [128, D], F32)[0]
            NE = 6
            winB0, _, _ = alloc("winB0", [128, 8, 2 * NE * 128], BF16)
            waB, _, _ = alloc("waB", [128, 12, 128], BF16)
            wxB, _, _ = alloc("wxB", [128, 12, 128], BF16)
            NSET = int(os.environ.get("KNSET", "3"))
            HDr = Ring("HD", [alloc("HD%d" % i, [128, 4], F32)[0] for i in range(3)])
            XCr = Ring("XC", [alloc("XC%d" % i, [128, 512], F32)[0] for i in range(NSET)])
            XCBr = Ring("XCB", [alloc("XCB%d" % i, [128, 512], BF16)[0] for i in range(2)])
            THRr = Ring("THR", [alloc("THR%d" % i, [128, 512], F32)[0] for i in range(NSET)])
            Ar = Ring("A", [alloc("A%d" % i, [128, 512], F32)[0] for i in range(NSET)])
            THIr = Ring("THI", [alloc("THI%d" % i, [128, 512], F32)[0] for i in range(NSET)])
            THGr = Ring("THG", [alloc("THG%d" % i, [128, 512], F32)[0] for i in range(NSET)])
            if doA:
                assert cur[0] <= A_EARLY_END, (cur[0], A_EARLY_END)
            winB1, _, _ = alloc("winB1", [128, 8, 2 * (12 - NE) * 128], BF16)
            woutB, _, _ = alloc("woutB", [128, 12, D], BF16)
            yTB, _, _ = alloc("yTB", [128, 12, TB], BF16)

            def wsel(part, c):
                if c < NE:
                    return winB0, part * NE * 128 + c * 128
                return winB1, part * (12 - NE) * 128 + (c - NE) * 128

            P.skip_fence = True
            if final_norm:
                P.dma("sp", lambda e: e.dma_start(out=NFW[0][:, :],
                                                  in_=norm_f_w_d.partition_broadcast(128)),
                      "nfw", writes=["nfw"])

            if os.environ.get("KMEM"):
                print("B mem end", cur[0], "top", top, "spare", top - cur[0], "REG0", REG0, "base", base)
            def pp_load(dst, src_ap, key):
                P.dma("sp", ncdma(lambda e: e.dma_start(out=dst, in_=src_ap)), key, writes=[key])
            pp_load(cw_pp[:, :].rearrange("p (k c) -> p k c", k=4),
                    b_conv_w_d.rearrange("k (c p) -> p k c", p=128), "cw")
            pp_load(cb_pp[:, :], b_conv_b_d.rearrange("(c p) -> p c", p=128), "cb")
            pp_load(ba_pp[:, :], b_ga_b_d.rearrange("(c p) -> p c", p=128), "ba")
            pp_load(bx_pp[:, :], b_gx_b_d.rearrange("(c p) -> p c", p=128), "bx")
            pp_load(k1_pp[:, :], b_lam_d.rearrange("(c p) -> p c", p=128), "k1")
            WSC = 3

            def winB_dma(part, lo, hi, hfkey, grp):
                wt, d0 = wsel(part, lo)
                c0 = part * BW + lo * 128
                w = (hi - lo) * 128
                P.dma("pool", lambda e: e.dma_start(
                    out=wt[:, :, d0:d0 + w],
                    in_=b_w_in_d[:, c0:c0 + w].rearrange("(k p) n -> p k n", p=128)),
                    ("winB", part, hfkey), reads=[("bch", grp - 1, i) for i in range(2)],
                    writes=[("winB", part, hfkey), ("bch", grp, part)])
            winB_dma(0, 0, WSC, 0, 0)
            winB_dma(1, 0, WSC, 0, 0)
            P.dma("pool", lambda e: e.dma_start(out=waB[:, :, :],
                                                in_=b_ga_w_d.rearrange("h i j -> i h j")),
                  "waB", reads=[("bch", 0, i) for i in range(2)], writes=["waB", ("bch", 1, 0)])
            P.dma("pool", lambda e: e.dma_start(out=wxB[:, :, :],
                                                in_=b_gx_w_d.rearrange("h i j -> i h j")),
                  "wxB", reads=[("bch", 0, i) for i in range(2)], writes=["wxB", ("bch", 1, 1)])
            winB_dma(0, WSC, NE, 1, 2)
            winB_dma(1, WSC, NE, 1, 2)
            P.skip_fence = False
            winB_dma(0, NE, 12, 2, 3)
            winB_dma(1, NE, 12, 2, 3)
            for q2 in range(2):
                P.dma("pool", lambda e, q2=q2: e.dma_start(
                    out=woutB[:, q2 * 6:(q2 + 1) * 6, :],
                    in_=b_w_out_d[q2 * 768:(q2 + 1) * 768, :].rearrange("(c p) d -> p c d", p=128)),
                    ("woutB", q2), reads=[("bch", 3, i) for i in range(2)], writes=[("woutB", q2)])
            P.skip_fence = True
            P.op("pool", lambda e: e.memset(halo[:, :, :], 0.0), writes=[("halo", c) for c in range(12)])
            P.op("pool", lambda e: e.memset(hstate[:, :], 0.0), writes=[("hstate", c) for c in range(12)])
            P.op("pool", lambda e: e.memset(quarter[:, :], 0.25), writes=["quarter"])
            P.op("act", lambda e: e.activation(out=k1_pp[:, :], in_=k1_pp[:, :], func=AF.Exp, scale=-1.0),
                 reads=["k1"], writes=["k1"])
            P.op("act", lambda e: e.activation(out=k1_pp[:, :], in_=k1_pp[:, :], func=AF.Ln, bias=1.0),
                 reads=["k1"], writes=["k1"])
            P.op("dve", lambda e: e.tensor_scalar(out=k2_pp[:, :], in0=k1_pp[:, :], scalar1=-8.0,
                                                  scalar2=None, op0=ALU.mult),
                 reads=["k1"], writes=["k2"])
            P.op("dve", lambda e: e.tensor_scalar(out=k1_pp[:, :], in0=k1_pp[:, :], scalar1=-4.0,
                                                  scalar2=None, op0=ALU.mult),
                 reads=["k1"], writes=["k1"])
            P.op("dve", lambda e: e.tensor_scalar(out=ba_pp[:, :], in0=ba_pp[:, :], scalar1=0.5,
                                                  scalar2=None, op0=ALU.mult),
                 reads=["ba"], writes=["ba"])
            P.op("dve", lambda e: e.tensor_scalar(out=bx_pp[:, :], in0=bx_pp[:, :], scalar1=0.5,
                                                  scalar2=None, op0=ALU.mult),
                 reads=["bx"], writes=["bx"])
            P.skip_fence = False

            def st1(T):
                c, hf = T["c"], T["hf"]
                for nm, ring in (("XC", XCr), ("XCB", XCBr), ("THR", THRr), ("A", Ar),
                                 ("THI", THIr), ("THG", THGr)):
                    T[nm], T["k" + nm] = ring.next()
                bX, bkX = PSB["chk"].next()
                T["bX"], T["bkX"] = bX, bkX
                wtx, dx0 = wsel(0, c)
                wtg, dg0 = wsel(1, c)
                for dk in range(8):
                    P.op("pe", lambda e, bX=bX, dk=dk, hT=CUR["hT"], wtx=wtx, dx0=dx0: e.matmul(
                        bX[:, :], lhsT=wtx[:, dk, dx0:dx0 + 128], rhs=hT[:, dk, :],
                        start=(dk == 0), stop=(dk == 7)),
                        reads=CUR["hk"] + [("winB", 0, hf)], writes=[bkX])
                bG, bkG = PSB["chk"].next()
                T["bG"], T["bkG"] = bG, bkG
                for dk in range(8):
                    P.op("pe", lambda e, bG=bG, dk=dk, hT=CUR["hT"], wtg=wtg, dg0=dg0: e.matmul(
                        bG[:, :], lhsT=wtg[:, dk, dg0:dg0 + 128],
                        rhs=hT[:, dk, :], start=(dk == 0), stop=(dk == 7)),
                        reads=CUR["hk"] + [("winB", 1, hf)], writes=[bkG])
                P.op("act", lambda e, XC=T["XC"], bX=bX, c=c: e.activation(
                    out=XC[:, :], in_=bX[:, :], func=AF.Identity,
                    scale=cw_pp[:, 36 + c:37 + c], bias=cb_pp[:, c:c + 1]),
                    reads=[bkX, "cw", "cb"], writes=[T["kXC"]])
                for k in (2, 1, 0):
                    sh = 3 - k
                    P.op("dve", lambda e, XC=T["XC"], bX=bX, k=k, sh=sh, c=c: e.scalar_tensor_tensor(
                        out=XC[:, sh:512], in0=bX[:, 0:512 - sh],
                        scalar=cw_pp[:, k * 12 + c:k * 12 + c + 1], in1=XC[:, sh:512],
                        op0=ALU.mult, op1=ALU.add),
                        reads=[bkX, T["kXC"], "cw"], writes=[T["kXC"]])
                    if os.environ.get("KHEAD", "pool") == "pool":
                        hd, khd = HDr.next()
                        P.op("pool", lambda e, hd=hd, k=k, sh=sh, c=c: e.tensor_scalar(
                            out=hd[:, 0:sh], in0=halo[:, c, 4 - sh:4],
                            scalar1=cw_pp[:, k * 12 + c:k * 12 + c + 1], scalar2=None, op0=ALU.mult),
                            reads=[("halo", c), "cw"], writes=[khd])
                        P.op("pool", lambda e, hd=hd, XC=T["XC"], sh=sh: e.tensor_tensor(
                            out=XC[:, 0:sh], in0=XC[:, 0:sh], in1=hd[:, 0:sh], op=ALU.add),
                            reads=[khd, T["kXC"]], writes=[T["kXC"]])
                    else:
                        P.op("dve", lambda e, XC=T["XC"], k=k, sh=sh, c=c: e.scalar_tensor_tensor(
                            out=XC[:, 0:sh], in0=halo[:, c, 4 - sh:4],
                            scalar=cw_pp[:, k * 12 + c:k * 12 + c + 1], in1=XC[:, 0:sh],
                            op0=ALU.mult, op1=ALU.add),
                            reads=[("halo", c), T["kXC"], "cw"], writes=[T["kXC"]])
                if os.environ.get("KHALO", "act") == "act":
                    P.op("act", lambda e, bX=bX, c=c: e.activation(
                        out=halo[:, c, :], in_=bX[:, 508:512], func=AF.Copy),
                        reads=[bkX], writes=[("halo", c)])
                else:
                    P.op("dve", lambda e, bX=bX, c=c: e.tensor_copy(out=halo[:, c, :], in_=bX[:, 508:512]),
                         reads=[bkX], writes=[("halo", c)])
                if os.environ.get("KCAST", "act") == "pool":
                    P.op("pool", lambda e, XCB=T["XCB"], XC=T["XC"]: e.tensor_copy(out=XCB[:, :], in_=XC[:, :]),
                         reads=[T["kXC"]], writes=[T["kXCB"]])
                else:
                    P.op("act", lambda e, XCB=T["XCB"], XC=T["XC"]: e.activation(
                        out=XCB[:, :], in_=XC[:, :], func=AF.Copy),
                        reads=[T["kXC"]], writes=[T["kXCB"]])
                bR, bkR = PSB["chk"].next()
                T["bR"], T["bkR"] = bR, bkR
                P.op("pe", lambda e, bR=bR, XCB=T["XCB"], c=c: e.matmul(
                    bR[:, :], lhsT=waB[:, c, :], rhs=XCB[:, :], start=True, stop=True),
                    reads=["waB", T["kXCB"]], writes=[bkR])
                bI, bkI = PSB["chk"].next()
                T["bI"], T["bkI"] = bI, bkI
                P.op("pe", lambda e, bI=bI, XCB=T["XCB"], c=c: e.matmul(
                    bI[:, :], lhsT=wxB[:, c, :], rhs=XCB[:, :], start=True, stop=True),
                    reads=["wxB", T["kXCB"]], writes=[bkI])
                P.op("act", lambda e, THG=T["THG"], bG=bG: e.activation(
                    out=THG[:, :], in_=bG[:, :], func=AF.Tanh, scale=0.5),
                    reads=[bkG], writes=[T["kTHG"]])
                P.op("dve", lambda e, THG=T["THG"], bG=bG: e.scalar_tensor_tensor(
                    out=THG[:, :], in0=THG[:, :], scalar=1.0, in1=bG[:, :],
                    op0=ALU.add, op1=ALU.mult),
                    reads=[T["kTHG"], bkG], writes=[T["kTHG"]])

            def st2(T):
                c = T["c"]
                P.op("act", lambda e, THR=T["THR"], bR=T["bR"], c=c: e.activation(
                    out=THR[:, :], in_=bR[:, :], func=AF.Tanh, scale=0.5, bias=ba_pp[:, c:c + 1]),
                    reads=[T["bkR"], "ba"], writes=[T["kTHR"]])
                P.op("act", lambda e, THI=T["THI"], bI=T["bI"], c=c: e.activation(
                    out=THI[:, :], in_=bI[:, :], func=AF.Tanh, scale=0.5, bias=bx_pp[:, c:c + 1]),
                    reads=[T["bkI"], "bx"], writes=[T["kTHI"]])
                P.op("act", lambda e, At=T["A"], THR=T["THR"], c=c: e.activation(
                    out=At[:, :], in_=THR[:, :], func=AF.Exp,
                    scale=k1_pp[:, c:c + 1], bias=k1_pp[:, c:c + 1]),
                    reads=[T["kTHR"], "k1"], writes=[T["kA"]])
                e2eng = os.environ.get("KE2", "dve")
                if e2eng == "act":
                    P.op("act", lambda e, THR=T["THR"], c=c: e.activation(
                        out=THR[:, :], in_=THR[:, :], func=AF.Exp,
                        scale=k2_pp[:, c:c + 1], bias=k2_pp[:, c:c + 1]),
                        reads=[T["kTHR"], T["kA"], "k2"], writes=[T["kTHR"]])
                else:
                    P.op(e2eng, lambda e, THR=T["THR"], At=T["A"]: e.tensor_tensor(
                        out=THR[:, :], in0=At[:, :], in1=At[:, :], op=ALU.mult),
                        reads=[T["kA"]], writes=[T["kTHR"]])
                P.op("dve", lambda e, THI=T["THI"], XC=T["XC"]: e.scalar_tensor_tensor(
                    out=THI[:, :], in0=THI[:, :], scalar=1.0, in1=XC[:, :],
                    op0=ALU.add, op1=ALU.mult),
                    reads=[T["kTHI"], T["kXC"]], writes=[T["kTHI"]])

            def st3(T):
                c = T["c"]
                P.op("act", lambda e, Et=T["THR"]: e.activation(
                    out=Et[:, :], in_=Et[:, :], func=AF.Sqrt, scale=-0.25, bias=quarter[:, 0:1]),
                    reads=[T["kTHR"], "quarter"], writes=[T["kTHR"]])
                P.op(os.environ.get("KBENG", "dve"), lambda e, THI=T["THI"], Et=T["THR"]: e.tensor_tensor(
                    out=THI[:, :], in0=THI[:, :], in1=Et[:, :], op=ALU.mult),
                    reads=[T["kTHI"], T["kTHR"]], writes=[T["kTHI"]])
                P.op("dve", lambda e, Ht=T["XC"], At=T["A"], THI=T["THI"], c=c: e.tensor_tensor_scan(
                    out=Ht[:, :], data0=At[:, :], data1=THI[:, :], initial=hstate[:, c:c + 1],
                    op0=ALU.mult, op1=ALU.add),
                    reads=[T["kA"], T["kTHI"], ("hstate", c)], writes=[T["kXC"]])
                P.op("pool", lambda e, Ht=T["XC"], c=c: e.tensor_copy(out=hstate[:, c:c + 1], in_=Ht[:, 511:512]),
                     reads=[T["kXC"]], writes=[("hstate", c)])
                P.op("dve", lambda e, THG=T["THG"], Ht=T["XC"], c=c: e.scalar_tensor_tensor(
                    out=yTB[:, c, :], in0=Ht[:, :], scalar=0.5, in1=THG[:, :],
                    op0=ALU.mult, op1=ALU.mult),
                    reads=[T["kXC"], T["kTHG"]], writes=[("yTB", c)])


            def phase3(tb, c_lo=0, c_hi=12, last=True):
                for j in range(4):
                    jt = tb * 4 + j
                    for dh in range(2):
                        bank, bk = PSB["blk"].next()
                        for c in range(c_lo, c_hi):
                            P.op("pe", lambda e, bank=bank, c=c, j=j, dh=dh: e.matmul(
                                bank[:, :], lhsT=yTB[:, c, j * 128:(j + 1) * 128],
                                rhs=woutB[:, c, dh * 512:(dh + 1) * 512],
                                start=(c == c_lo), stop=(c == c_hi - 1)),
                                reads=[("yTB", c), ("woutB", c // 6)], writes=[bk])
                        residual_add(bank, bk, jt, dh)
                if final_norm and last:
                    for j in range(4):
                        finalize(tb * 4 + j)

            def start_block(tb):
                P.skip_fence = (tb == 0)
                phase0(1, tb)
                P.skip_fence = False

            if os.environ.get("KXTB", "1") == "1":
                PSB["blk"] = PSP
                PSB["chk"] = PSC
            allc = [dict(c=c, hf=(0 if c < WSC else (1 if c < NE else 2)), tb=tb)
                    for tb in range(NTB) for c in range(12)]
            NEARLY = int(os.environ.get("KNEARLY", "3")) if doA else 0
            G = len(allc)
            if os.environ.get("KXTB", "1") == "1":
                for g in range(G + 2):
                    if g < G:
                        if allc[g]["c"] == 0:
                            start_block(allc[g]["tb"])
                        P.skip_fence = g < NEARLY
                        st1(allc[g])
                        P.skip_fence = False
                    if 0 <= g - 1 < G:
                        P.skip_fence = (g - 1) < NEARLY
                        st2(allc[g - 1])
                        P.skip_fence = False
                    if 0 <= g - 2 < G:
                        st3(allc[g - 2])
                        tb3, c3 = allc[g - 2]["tb"], allc[g - 2]["c"]
                        SPL = int(os.environ.get("KSPLIT", "9"))
                        if SPL and tb3 == NTB - 1:
                            if c3 == SPL - 1:
                                phase3(tb3, 0, SPL, last=False)
                            elif c3 == 11:
                                phase3(tb3, SPL, 12, last=True)
                        elif c3 == 11:
                            phase3(tb3)
            else:
                for tb in range(NTB):
                    start_block(tb)
                    chunks = allc[tb * 12:(tb + 1) * 12]
                    for i in range(12 + 2):
                        if i < 12:
                            st1(chunks[i])
                        if 0 <= i - 1 < 12:
                            st2(chunks[i - 1])
                        if 0 <= i - 2 < 12:
                            st3(chunks[i - 2])
                    phase3(tb)
    except _StopBuild:
        pass

    if final_norm and NFW[0] is None:
        NFW[0] = nc.alloc_sbuf_tensor_at("nfw_b", [128, D], F32, offset=hT_off)
        P.dma("sp", lambda e: e.dma_start(out=NFW[0][:, :], in_=norm_f_w_d.partition_broadcast(128)),
              "nfw", writes=[("hT", 0, j) for j in range(4)] + ["nfw"])
    for jt in range(NT):
        if jt not in done_tiles:
            finalize(jt)
    P.emit()
    _EST[0] = P.est_us
    _EST.append(P)
    return nc


_CACHE = {}


def _get(layers, final_norm):
    key = (layers, final_norm)
    if key not in _CACHE:
        _CACHE[key] = build_program(layers, final_norm)
    return _CACHE[key]


A_NAMES = ["a_w_in", "a_ln_w", "a_ln_b", "a_w_s", "a_b_s", "a_w_out"]
B_NAMES = ["b_w_in", "b_conv_w", "b_conv_b", "b_gate_a_w", "b_gate_a_b", "b_gate_x_w",
           "b_gate_x_b", "b_lambda", "b_w_out"]


def _run(layers, final_norm, x, w):
    nc = _get(layers, final_norm)
    names = ["norm_w", "norm_f_w"]
    if "A" in layers:
        names += A_NAMES
    if "B" in layers:
        names += B_NAMES
    in_maps = []
    for b in range(8):
        m = {"x": np.ascontiguousarray(x[b])}
        for n in names:
            m[n] = w[n]
        in_maps.append(m)
    res = run_bass_kernel_spmd(nc, in_maps, core_ids=list(range(8)))
    return np.stack([r["out"] for r in res.results], axis=0)


FUSED = True


def kernel(**inputs):
    w = {}
    for k, v in inputs.items():
        if k == "x":
            continue
        a = np.asarray(v, dtype=np.float32)
        if k not in ("norm_w", "norm_f_w"):
            a = a[0]
        w[k] = np.ascontiguousarray(a)
    x = np.asarray(inputs["x"], dtype=np.float32)
    if FUSED:
        return _run("AB", True, x, w)
    x1 = _run("A", False, x, w)
    return _run("B", True, x1, w)
```

```python
import os
import numpy as np
import concourse.bass as bass
import concourse.mybir as mybir
from concourse.bass_utils import run_bass_kernel_spmd

F32 = mybir.dt.float32
BF16 = mybir.dt.bfloat16
AF = mybir.ActivationFunctionType
ALU = mybir.AluOpType

S = 2048
D = 1024
NT = 16
TB = 512
NTB = 4
AW = 2048
BW = 1536
RMS_EPS = 1e-6
LN_EPS = 1e-5

ENGS = ("pe", "act", "dve", "pool", "sp")
_EST = [None]


class Op:
    __slots__ = ("eng", "fn", "deps", "is_dma", "sem_key", "signal", "count", "idx", "n", "tbl",
                 "nbytes", "pos", "sdeps", "name")

    def __init__(self, eng, fn, is_dma=False, sem_key=None, n=512, tbl=None, nbytes=0):
        self.eng = eng
        self.fn = fn
        self.deps = set()
        self.is_dma = is_dma
        self.sem_key = sem_key
        self.signal = False
        self.count = 0
        self.idx = -1
        self.n = n
        self.tbl = tbl
        self.nbytes = nbytes
        self.pos = -1
        self.sdeps = ()
        self.name = None


class _Probe:
    def __init__(self):
        self.name = None
        self.out = None
        self.func = None

    def __getattr__(self, name):
        def call(*args, **kwargs):
            self.name = name
            self.out = kwargs.get("out", args[0] if args else None)
            self.func = kwargs.get("func")
            return None
        return call


_TBL = None


def _tbl_of(func):
    global _TBL
    if _TBL is None:
        _TBL = {AF.Tanh: ("exp", "gelu"), AF.Exp: ("exp",), AF.Gelu_apprx_tanh: ("gelu",),
                AF.Sqrt: ("sqrt",), AF.Ln: ("ln",)}
    return _TBL.get(func)


class Prog:
    def __init__(self, nc, same_engine_sync=True, reorder=True):
        self.nc = nc
        self.ops = []
        self.last_w = {}
        self.readers = {}
        self.same_engine_sync = same_engine_sync
        self.reorder = reorder
        self.final_dma_keys = []
        self.fence_op = None
        self.fence_start = 0
        self.skip_fence = False
        self.est_us = None

    def _add(self, op, reads, writes):
        op.idx = len(self.ops)
        if self.fence_op is not None and not self.skip_fence:
            op.deps.add(self.fence_op)
        excl = [k for k in reads if isinstance(k, tuple) and k[0] == "ps"]
        if excl:
            reads = [k for k in reads if k not in excl]
            writes = list(writes) + [k for k in excl if k not in writes]
        for k in reads:
            w = self.last_w.get(k)
            if w is not None:
                op.deps.add(w)
        for k in writes:
            w = self.last_w.get(k)
            if w is not None:
                op.deps.add(w)
            for r in self.readers.get(k, ()):
                op.deps.add(r)
        for k in reads:
            self.readers.setdefault(k, []).append(op.idx)
        for k in writes:
            self.last_w[k] = op.idx
            self.readers[k] = []
        op.deps.discard(op.idx)
        self.ops.append(op)
        return op

    @staticmethod
    def _probe(fn):
        pr = _Probe()
        try:
            fn(pr)
        except Exception:
            pass
        n, nbytes = 512, 1 << 20
        if pr.out is not None:
            shp = tuple(pr.out.shape)
            n = 1
            for d in shp[1:]:
                n *= int(d)
            nbytes = n * int(shp[0]) * (4 if pr.out.dtype == F32 else 2)
            if pr.name == "tensor_tensor_scan":
                n *= 2
        return n, nbytes, _tbl_of(pr.func), pr.name

    def op(self, eng, fn, reads=(), writes=()):
        n, _, tbl, name = self._probe(fn)
        o = Op(eng, fn, n=n, tbl=tbl)
        o.name = name
        return self._add(o, reads, writes)

    def dma(self, eng, fn, sem_key, reads=(), writes=(), final=False):
        _, nbytes, _, _ = self._probe(fn)
        o = self._add(Op(eng, fn, is_dma=True, sem_key=sem_key, nbytes=nbytes), reads, writes)
        if final:
            if sem_key not in self.final_dma_keys:
                self.final_dma_keys.append(sem_key)
        return o

    def fence(self, fn):
        o = Op("pool", fn, n=8)
        o.idx = len(self.ops)
        o.deps = set(range(self.fence_start, o.idx))
        self.ops.append(o)
        self.fence_op = o.idx
        self.fence_start = o.idx
        return o

    @staticmethod
    def _dur(o):
        n = o.n
        k = float(os.environ.get("KSCALE_" + o.eng.upper(), "1"))
        if k != 1.0 and not o.is_dma:
            o2 = Op(o.eng, None, n=o.n); o2.name = o.name
            return k * Prog._dur0(o2)
        return Prog._dur0(o)

    @staticmethod
    def _dur0(o):
        n = o.n
        if o.is_dma:
            return 1.2 if o.eng == "pool" else 0.1
        if o.eng == "pe":
            return 0.03 + n / 2400.0
        if o.eng == "act":
            return 0.25 + n / 1200.0
        if o.eng == "dve":
            return 0.16 + n / 960.0
        if o.eng == "pool":
            if o.name == "tensor_tensor":
                return 0.15 + n * 0.00175
            if o.name == "tensor_copy":
                return 0.2 + n * 0.0033
            return 0.18 + n * 0.0022
        return 0.1

    def _schedule(self):
        ops = self.ops
        per_eng = {e: [o.idx for o in ops if o.eng == e] for e in ENGS}
        if not self.reorder:
            for e in ENGS:
                for p, i in enumerate(per_eng[e]):
                    ops[i].pos = p
            return per_eng
        INF = float("inf")
        PRIO = int(os.environ.get("KPRIO", "1"))
        WINI = int(os.environ.get("KWIN", "0"))
        TWAIT = float(os.environ.get("KTWAIT", "0.5"))
        nops = len(ops)
        fin = [None] * nops
        succ = [[] for _ in range(nops)]
        left = [0] * nops
        for o in ops:
            left[o.idx] = len(o.deps)
            for d in o.deps:
                succ[d].append(o.idx)
        rdy = [0.0] * nops
        cp = [0.0] * nops
        if os.environ.get("KCP", "1") == "1":
            for i in range(nops - 1, -1, -1):
                c = 0.0
                for sidx in succ[i]:
                    if cp[sidx] > c:
                        c = cp[sidx]
                cp[i] = c + self._dur(ops[i]) + (2.0 if ops[i].is_dma else 0.0)
            if os.environ.get("KSEED"):
                import random
                rnd = random.Random(int(os.environ["KSEED"]))
                rr = float(os.environ.get("KNOISE", "0.05"))
                cp = [c * (1.0 + rr * (2 * rnd.random() - 1)) for c in cp]
        avail = {e: [] for e in ENGS}
        for o in ops:
            if left[o.idx] == 0:
                avail[o.eng].append(o.idx)
        t_e = {e: 0.0 for e in ENGS}
        npend = {e: len(per_eng[e]) for e in ENGS}
        order = {e: [] for e in ENGS}
        cur_tbl = [None]
        dma_free = [0.0]
        remaining = nops
        guard = 0
        while remaining:
            guard += 1
            assert guard < 4000000, "scheduler stuck"
            e = min((x for x in ENGS if npend[x]), key=lambda x: t_e[x])
            t = t_e[e]
            best = None
            best_key = None
            min_r = INF
            lim = (min(avail[e]) + WINI) if (avail[e] and WINI) else None
            for i in avail[e]:
                if lim is not None and i > lim:
                    continue
                r = rdy[i]
                if r < min_r:
                    min_r = r
                if r <= t:
                    o = ops[i]
                    pen = 0
                    if e == "act" and o.tbl is not None and cur_tbl[0] is not None and cur_tbl[0] not in o.tbl:
                        pen = 1
                    key = (pen, -cp[i], i) if PRIO == 0 else ((-cp[i] + 3.0 * pen, i) if PRIO == 1 else (0, -cp[i], i))
                    if best_key is None or key < best_key:
                        best_key = key
                        best = i
            if best is None:
                if min_r < INF:
                    t_e[e] = max(t, min_r)
                else:
                    others = [t_e[x] for x in ENGS if x != e and npend[x] and t_e[x] > t]
                    if others:
                        t_e[e] = min(others) + 1e-3
                    else:
                        others = [t_e[x] for x in ENGS if x != e and npend[x]]
                        assert others, "dependency deadlock in scheduler"
                        t_e[e] = max(others) + 1e-3
                continue
            o = ops[best]
            if (e == "act" and TWAIT > 0 and o.tbl is not None and cur_tbl[0] is not None
                    and cur_tbl[0] not in o.tbl):
                soon = [rdy[i] for i in avail[e]
                        if ops[i].tbl is not None and cur_tbl[0] in ops[i].tbl and t < rdy[i] <= t + TWAIT]
                if soon:
                    t_e[e] = min(soon)
                    continue
            start = t
            if e == "act" and o.tbl is not None:
                if cur_tbl[0] is None or cur_tbl[0] not in o.tbl:
                    start += float(os.environ.get("KTBL", "1.3"))
                    self.n_tbl = getattr(self, "n_tbl", 0) + 1
                    cur_tbl[0] = o.tbl[0]
            d = self._dur(o)
            t_e[e] = start + d
            if o.is_dma:
                s0 = max(start + d, dma_free[0])
                dma_free[0] = s0 + o.nbytes / 120e3
                fin[best] = dma_free[0] + 2.0
            else:
                fin[best] = start + d
            o.pos = len(order[e])
            order[e].append(best)
            avail[e].remove(best)
            npend[e] -= 1
            remaining -= 1
            fb = fin[best]
            for sidx in succ[best]:
                so = ops[sidx]
                lat = 0.0 if (so.eng == e and not o.is_dma) else float(os.environ.get("KLAT", "0.25"))
                if fb + lat > rdy[sidx]:
                    rdy[sidx] = fb + lat
                left[sidx] -= 1
                if left[sidx] == 0:
                    avail[so.eng].append(sidx)
        self.est_us = max(f for f in fin if f is not None)
        self.fin = fin
        return order

    def emit(self):
        nc = self.nc
        ops = self.ops
        order = self._schedule()
        for o in ops:
            if o.is_dma:
                o.signal = True
            best = {}
            for d in o.deps:
                p = ops[d]
                if p.is_dma:
                    k = ("dma", p.sem_key)
                else:
                    if p.eng == o.eng and not o.is_dma:
                        if o.eng == "pe" or not self.same_engine_sync:
                            continue
                    k = ("eng", p.eng)
                q = best.get(k)
                if q is None or ops[q].pos < p.pos:
                    best[k] = d
            o.sdeps = tuple(best.values())
            for d in o.sdeps:
                ops[d].signal = True
        eng_cnt = {e: 0 for e in ENGS}
        dma_cnt = {}
        for e in ENGS:
            for i in order[e]:
                o = ops[i]
                if not o.signal:
                    continue
                if o.is_dma:
                    dma_cnt[o.sem_key] = dma_cnt.get(o.sem_key, 0) + 16
                    o.count = dma_cnt[o.sem_key]
                else:
                    eng_cnt[e] += 1
                    o.count = eng_cnt[e]
        sems = {}
        for e in ENGS:
            if eng_cnt[e] > 0:
                sems[("eng", e)] = nc.alloc_semaphore("s_" + e)
        for i, k in enumerate(dma_cnt):
            sems[("dma", k)] = nc.alloc_semaphore("d_%d" % i)
        self.n_sems = len(sems)

        def sem_of(p):
            return sems[("dma", p.sem_key)] if p.is_dma else sems[("eng", p.eng)]

        final_waits = [(sems[("dma", k)], dma_cnt[k]) for k in self.final_dma_keys]

        def run(eng_name, eng):
            waited = {}
            for i in order[eng_name]:
                o = ops[i]
                for d in o.sdeps:
                    p = ops[d]
                    s = sem_of(p)
                    if waited.get(id(s), 0) < p.count:
                        eng.wait_ge(s, p.count)
                        waited[id(s)] = p.count
                ins = o.fn(eng)
                if o.signal:
                    ins.then_inc(sem_of(o), 16 if o.is_dma else 1)
            if eng_name == "sp":
                for s, c in final_waits:
                    eng.wait_ge(s, c)

        with nc.Block() as block:
            @block.sync
            def _(e):
                run("sp", e)

            @block.tensor
            def _(e):
                run("pe", e)

            @block.scalar
            def _(e):
                run("act", e)

            @block.vector
            def _(e):
                run("dve", e)

            @block.gpsimd
            def _(e):
                run("pool", e)


class _StopBuild(Exception):
    pass


class Ring:
    def __init__(self, name, tiles, keys=None):
        self.name = name
        self.tiles = tiles
        self.keys = keys if keys is not None else [(name, k) for k in range(len(tiles))]
        self.i = 0

    def next(self):
        k = self.i % len(self.tiles)
        self.i += 1
        return self.tiles[k], self.keys[k]


def build_program(layers="AB", final_norm=True):
    nc = bass.Bass("TRN2", target_bir_lowering=False)
    P = Prog(nc)
    doA = "A" in layers
    stop = os.environ.get("KSTOP", "")

    def chk(tag):
        if stop == tag:
            raise _StopBuild()
    doB = "B" in layers

    def din(name, shape):
        return nc.dram_tensor(name, list(shape), F32, kind="ExternalInput").ap()

    x_d = din("x", [S, D])
    norm_w_d = din("norm_w", [2, D])
    norm_f_w_d = din("norm_f_w", [D])
    if doA:
        a_w_in_d = din("a_w_in", [D, 3 * AW])
        a_ln_w_d = din("a_ln_w", [AW])
        a_ln_b_d = din("a_ln_b", [AW])
        a_w_s_d = din("a_w_s", [8, 128, 128])
        a_b_s_d = din("a_b_s", [8, 128])
        a_w_out_d = din("a_w_out", [AW, D])
    if doB:
        b_w_in_d = din("b_w_in", [D, 2 * BW])
        b_conv_w_d = din("b_conv_w", [4, BW])
        b_conv_b_d = din("b_conv_b", [BW])
        b_ga_w_d = din("b_gate_a_w", [12, 128, 128])
        b_ga_b_d = din("b_gate_a_b", [BW])
        b_gx_w_d = din("b_gate_x_w", [12, 128, 128])
        b_gx_b_d = din("b_gate_x_b", [BW])
        b_lam_d = din("b_lambda", [BW])
        b_w_out_d = din("b_w_out", [BW, D])
    out_d = nc.dram_tensor("out", [S, D], F32, kind="ExternalOutput").ap()

    base = (nc.sbuf_base + 63) // 64 * 64
    top = nc.sbuf_top
    cur = [base]

    def alloc(name, shape, dt, at=None):
        nbytes = int(np.prod(shape[1:])) * (4 if dt == F32 else 2)
        nbytes = (nbytes + 63) // 64 * 64
        if at is None:
            off = cur[0]
            cur[0] += nbytes
        else:
            off = at
        if off + nbytes > top and os.environ.get("KNOMEM"):
            off = base
        assert off + nbytes <= top, (name, off, nbytes, top)
        return nc.alloc_sbuf_tensor_at(name, list(shape), dt, offset=off), off, nbytes

    x_sb, _, _ = alloc("x_sb", [128, NT, D], F32)
    ident, _, _ = alloc("ident", [128, 128], F32)
    NHT = int(os.environ.get("KNHT", "1"))
    hTs = []
    for _i in range(NHT):
        _t, _o, _ = alloc("hT%d" % _i, [128, 8, TB], BF16)
        hTs.append(_t)
        if _i == 0:
            hT_off = _o
    CUR = {"hT": hTs[0], "hk": [("hT", 0, j) for j in range(4)]}
    xs = [alloc("xs%d" % i, [128, D], F32)[0] for i in range(2)]
    nw_pp, _, _ = alloc("nw_pp", [128, 16], F32)
    ss_t, _, _ = alloc("ss_t", [128, NT], F32)
    ms_t, _, _ = alloc("ms_t", [128, NT], F32)
    rstd_t, _, _ = alloc("rstd_t", [128, NT], F32)
    neg_half, _, _ = alloc("neg_half", [128, 1], F32)
    REG0 = cur[0]

    psum = [nc.alloc_psum_tensor("ps%d" % i, [128, 512], F32) for i in range(8)]
    NPS = int(os.environ.get("KNPS", "8"))
    PS = Ring("ps", [psum[i % 8] for i in range(NPS)])
    PSC = Ring("ps", psum[0:6], keys=[("ps", k) for k in range(6)])
    PSP = Ring("ps", psum[6:8], keys=[("ps", k) for k in (6, 7)])
    PSB = {"blk": PS, "chk": PS}
    XS = Ring("xs", xs)

    def ncdma(fn):
        def f(e):
            with nc.allow_non_contiguous_dma(reason="small parameter relayout"):
                return fn(e)
        return f

    for tb in range(NTB):
        P.dma("sp", lambda e, tb=tb: e.dma_start(
            out=x_sb[:, tb * 4:(tb + 1) * 4, :],
            in_=x_d[tb * TB:(tb + 1) * TB, :].rearrange("(j p) d -> p j d", p=128)),
            ("xl", tb), writes=[("x", tb * 4 + j) for j in range(4)] + ["xchain"])
    P.dma("sp", ncdma(lambda e: e.dma_start(out=nw_pp[:, :].rearrange("p (l k) -> p l k", l=2),
                                            in_=norm_w_d.rearrange("l (k p) -> p l k", p=128))),
          "nw", writes=["nw"])
    P.op("pool", lambda e: e.memset(ident[:, :], 0.0), writes=["ident"])
    P.op("pool", lambda e: e.affine_select(out=ident[:, :], in_=ident[:, :], pattern=[[-1, 128]],
                                           compare_op=ALU.not_equal, fill=1.0, base=0,
                                           channel_multiplier=1),
         reads=["ident"], writes=["ident"])
    P.op("pool", lambda e: e.memset(neg_half[:, :], -0.5), writes=["neg_half"])

    evac_flip = [0]

    def rms_stats(jt, xkey):
        junk, jk = XS.next()
        P.op("act", lambda e: e.activation(out=junk[:, :], in_=x_sb[:, jt, :], func=AF.Square,
                                           accum_out=ss_t[:, jt:jt + 1]),
             reads=[xkey], writes=[jk, ("ss", jt)])
        P.op("dve", lambda e: e.tensor_scalar(out=ms_t[:, jt:jt + 1], in0=ss_t[:, jt:jt + 1],
                                              scalar1=1.0 / D, scalar2=RMS_EPS,
                                              op0=ALU.mult, op1=ALU.add),
             reads=[("ss", jt)], writes=[("ms", jt)])
        P.op("pool", lambda e: e.tensor_tensor(out=rstd_t[:, jt:jt + 1], in0=ms_t[:, jt:jt + 1],
                                               in1=neg_half[:, :], op=ALU.pow),
             reads=[("ms", jt), "neg_half"], writes=[("rstd", jt)])

    def phase0(layer, tb):
        hb = tb % NHT
        CUR["hT"] = hTs[hb]
        CUR["hk"] = [("hT", hb, j) for j in range(4)]
        hT = hTs[hb]
        for j in range(4):
            jt = tb * 4 + j
            xkey = ("x", jt)
            rms_stats(jt, xkey)
            xt, xk = XS.next()
            P.op("act", lambda e, xt=xt, jt=jt: e.activation(out=xt[:, :], in_=x_sb[:, jt, :],
                                                             func=AF.Copy,
                                                             scale=rstd_t[:, jt:jt + 1]),
                 reads=[xkey, ("rstd", jt)], writes=[xk])
            for half in range(2):
                bank, bk = PSB["blk"].next()
                for q in range(4):
                    dk = half * 4 + q
                    P.op("pe", lambda e, bank=bank, q=q, dk=dk, xt=xt: e.transpose(
                        out=bank[:, q * 128:(q + 1) * 128], in_=xt[:, dk * 128:(dk + 1) * 128],
                        identity=ident[:, :]),
                        reads=[xk, "ident"], writes=[bk])
                for q in range(4):
                    dk = half * 4 + q
                    col = layer * 8 + dk
                    dst = hT[:, dk, j * 128:(j + 1) * 128]
                    src = bank[:, q * 128:(q + 1) * 128]
                    _ev = os.environ.get("KEVAC%d" % layer, "dve" if layer == 0 else "alt")
                    if _ev == "dve" or (_ev == "alt" and (evac_flip[0] // 4) % 2 == 0):
                        P.op("dve", lambda e, dst=dst, src=src, col=col: e.tensor_scalar(
                            out=dst, in0=src, scalar1=nw_pp[:, col:col + 1], scalar2=None,
                            op0=ALU.mult),
                            reads=[bk, "nw"], writes=[("hT", hb, j)])
                    else:
                        P.op("act", lambda e, dst=dst, src=src, col=col: e.activation(
                            out=dst, in_=src, func=AF.Copy, scale=nw_pp[:, col:col + 1]),
                            reads=[bk, "nw"], writes=[("hT", hb, j)])
                    evac_flip[0] += 1


    def residual_add(bank, bk, jt, dh):
        P.op("dve", lambda e: e.tensor_tensor(out=x_sb[:, jt, dh * 512:(dh + 1) * 512],
                                              in0=x_sb[:, jt, dh * 512:(dh + 1) * 512],
                                              in1=bank[:, :], op=ALU.add),
             reads=[bk, ("x", jt)], writes=[("x", jt)])

    NFW = [None]
    done_tiles = set()
    B_EARLY_KEYS = (["cw", "cb", "ba", "bx", "k1", "k2", "quarter", "nfw", "waB", "wxB"]
                    + [("halo", c) for c in range(12)] + [("hstate", c) for c in range(12)]
                    + [("winB", p, h) for p in range(2) for h in range(2)]
                    + [("bch", g, i) for g in range(3) for i in range(2)]
                    + [(r, k) for r in ("HD", "XC", "XCB", "THR", "A", "THI", "THG") for k in range(4)])

    def finalize(jt):
        done_tiles.add(jt)
        if final_norm:
            nfw_b = NFW[0]
            rms_stats(jt, ("x", jt))
            P.op("dve", lambda e: e.scalar_tensor_tensor(
                out=x_sb[:, jt, :], in0=x_sb[:, jt, :], scalar=rstd_t[:, jt:jt + 1], in1=nfw_b[:, :],
                op0=ALU.mult, op1=ALU.mult),
                reads=[("x", jt), ("rstd", jt), "nfw"], writes=[("x", jt)])
            P.dma("sp", lambda e: e.dma_start(out=out_d[jt * 128:(jt + 1) * 128, :],
                                              in_=x_sb[:, jt, :]),
                  ("out", jt % 4), reads=[("x", jt)], final=True)
        else:
            P.dma("sp", lambda e: e.dma_start(out=out_d[jt * 128:(jt + 1) * 128, :],
                                              in_=x_sb[:, jt, :]),
                  ("out", jt % 4), reads=[("x", jt)], final=True)

    try:
        if doA:
            cur[0] = REG0
            lnw_pp, _, _ = alloc("lnw_pp", [128, 16], F32)
            stats, _, _ = alloc("stats", [128, 4, 24], F32)
            mv, _, _ = alloc("mv", [128, 4, 2], F32)
            lve, _, _ = alloc("lve", [128, 4], F32)
            lrstd, _, _ = alloc("lrstd", [128, 4], F32)
            nmr, _, _ = alloc("nmr", [128, 4], F32)
            stages = [alloc("stg%d" % i, [128, 8, 512], BF16)[0] for i in range(4)]
            biasT, _, _ = alloc("biasT", [128, 16, 128], F32)
            wcT, _, _ = alloc("wcT", [128, 8, 128], BF16)
            _gu0 = cur[0]
            GUr = Ring("GU", [alloc("GU%d" % i, [128, 512], F32)[0] for i in range(2)])
            THr = Ring("TH", [alloc("TH%d" % i, [128, 512], F32)[0] for i in range(2)])
            SSr = Ring("SS", [alloc("SS%d" % i, [128, 512], F32)[0] for i in range(2)])
            vn, vn_off, _ = alloc("vn", [128, 4, AW], BF16)
            A_EARLY_END = cur[0]
            woutA, _, _ = alloc("woutA", [128, 16, D], BF16)
            yT, _, _ = alloc("yT", [128, 16, TB], BF16)
            so = [_gu0]

            def salloc(name, shape, dt):
                t, off, nb = alloc(name, shape, dt, at=so[0])
                so[0] += nb
                return t
            ws_raw = salloc("ws_raw", [128, 8, 128], F32)
            wcT_f = salloc("wcT_f", [128, 8, 128], F32)
            lnb_row = salloc("lnb_row", [1, AW], F32)
            r_row = salloc("r_row", [1, 1024], F32)
            bs_row = salloc("bs_row", [1, 1024], F32)
            ones_row = salloc("ones_row", [1, 128], F32)
            ones_col = salloc("ones_col", [128, 1], F32)
            assert so[0] <= A_EARLY_END
            if os.environ.get("KMEM"):
                print("A mem end", cur[0], "top", top, "spare", top - cur[0])

            P.dma("sp", ncdma(lambda e: e.dma_start(out=lnw_pp[:, :],
                                                    in_=a_ln_w_d.rearrange("(k p) -> p k", p=128))),
                  "lnw", writes=["lnw"])
            P.dma("sp", lambda e: e.dma_start(out=ws_raw[:, :, :],
                                              in_=a_w_s_d.rearrange("g t s -> t g s")),
                  "ws", writes=["ws_raw"])
            P.dma("sp", lambda e: e.dma_start(out=lnb_row[:, :],
                                              in_=a_ln_b_d.rearrange("(o n) -> o n", o=1)),
                  "lnb", writes=["lnb_row"])
            P.dma("sp", lambda e: e.dma_start(out=bs_row[:, :],
                                              in_=a_b_s_d.rearrange("(o g) t -> o (g t)", o=1)),
                  "bs", writes=["bs_row"])
            P.op("pool", lambda e: e.memset(ones_row[:, :], 1.0), writes=["ones_row"])
            P.op("pool", lambda e: e.memset(ones_col[:, :], 1.0), writes=["ones_col"])
            P.op("pool", lambda e: e.affine_select(out=ws_raw[:, :, :], in_=ws_raw[:, :, :],
                                                   pattern=[[0, 8], [-1, 128]], compare_op=ALU.is_ge,
                                                   fill=0.0, base=0, channel_multiplier=1),
                 reads=["ws_raw"], writes=["ws_raw"])
            for half in range(2):
                bank, bk = PS.next()
                for q in range(4):
                    g = half * 4 + q
                    P.op("pe", lambda e, bank=bank, q=q, g=g: e.transpose(
                        out=bank[:, q * 128:(q + 1) * 128], in_=ws_raw[:, g, :], identity=ident[:, :]),
                        reads=["ws_raw", "ident"], writes=[bk])
                P.op("dve", lambda e, bank=bank, half=half: e.tensor_copy(
                    out=wcT_f[:, half * 4:(half + 1) * 4, :].rearrange("p g t -> p (g t)"),
                    in_=bank[:, :]), reads=[bk], writes=["wcT_f"])
            P.op("dve", lambda e: e.tensor_copy(out=wcT[:, :, :], in_=wcT_f[:, :, :]),
                 reads=["wcT_f"], writes=["wcT"])
            for half in range(2):
                bank, bk = PS.next()
                P.op("pe", lambda e, bank=bank, half=half: e.matmul(
                    bank[0:1, :], lhsT=ones_col[:, :],
                    rhs=wcT_f[:, half * 4:(half + 1) * 4, :].rearrange("p g t -> p (g t)"),
                    start=True, stop=True),
                    reads=["ones_col", "wcT_f"], writes=[bk])
                P.op("dve", lambda e, bank=bank, half=half: e.tensor_copy(
                    out=r_row[:, half * 512:(half + 1) * 512], in_=bank[0:1, :]),
                    reads=[bk], writes=["r_row"])
            for q4 in range(4):
                bank, bk = PS.next()
                for q in range(4):
                    cc = q4 * 4 + q
                    g = cc // 2
                    P.op("pe", lambda e, bank=bank, q=q, cc=cc, g=g: e.matmul(
                        bank[:, q * 128:(q + 1) * 128], lhsT=lnb_row[:, cc * 128:(cc + 1) * 128],
                        rhs=r_row[:, g * 128:(g + 1) * 128], start=True, stop=False),
                        reads=["lnb_row", "r_row"], writes=[bk])
                    P.op("pe", lambda e, bank=bank, q=q, cc=cc, g=g: e.matmul(
                        bank[:, q * 128:(q + 1) * 128], lhsT=ones_row[:, :],
                        rhs=bs_row[:, g * 128:(g + 1) * 128], start=False, stop=True),
                        reads=["ones_row", "bs_row"], writes=[bk])
                P.op("dve", lambda e, bank=bank, q4=q4: e.tensor_copy(
                    out=biasT[:, q4 * 4:(q4 + 1) * 4, :].rearrange("p c t -> p (c t)"),
                    in_=bank[:, :]), reads=[bk], writes=["biasT"])
            alias_keys = ["ws_raw", "wcT_f", "lnb_row", "r_row", "bs_row", "ones_row", "ones_col"]
            P.op("pool", lambda e: e.memset(ones_col[:, :], 1.0),
                 writes=alias_keys + [("vn", j, nb) for j in range(4) for nb in range(4)]
                 + [(r, k) for r in ("GU", "TH", "SS") for k in range(2)])
            chk("setup")

            stream = []
            for tb in range(NTB):
                for nb in range(4):
                    stream.append(AW + nb * 512)
                for cg in range(4):
                    stream.append(cg * 512)
                    stream.append(2 * AW + cg * 512)
            st_issued = [0]

            def issue_loads(upto):
                while st_issued[0] < min(upto, len(stream)):
                    i = st_issued[0]
                    col = stream[i]
                    s = i % 4
                    P.dma("pool", lambda e, s=s, col=col: e.dma_start(
                        out=stages[s][:, :, :],
                        in_=a_w_in_d[:, col:col + 512].rearrange("(k p) n -> p k n", p=128)),
                        ("stg", s), writes=[("stg", s)] + (["pchain"] if i < 3 else []))
                    st_issued[0] += 1
            st_used = [0]

            def next_stage():
                i = st_used[0]
                issue_loads(i + 1)
                st_used[0] += 1
                return stages[i % 4], ("stg", i % 4)

            issue_loads(3)
            for q4 in range(4):
                P.dma("pool", lambda e, q4=q4: e.dma_start(
                    out=woutA[:, q4 * 4:(q4 + 1) * 4, :],
                    in_=a_w_out_d[q4 * 512:(q4 + 1) * 512, :].rearrange("(c p) d -> p c d", p=128)),
                    ("woutA", q4), writes=[("woutA", q4), "pchain"])
            for tb in range(NTB):
                phase0(0, tb)
                chk("p0")
                for nb in range(4):
                    stg, sk = next_stage()
                    for j in range(4):
                        bank, bk = PS.next()
                        for dk in range(8):
                            P.op("pe", lambda e, bank=bank, stg=stg, dk=dk, j=j, hT=CUR["hT"]: e.matmul(
                                bank[:, :], lhsT=hT[:, dk, j * 128:(j + 1) * 128], rhs=stg[:, dk, :],
                                start=(dk == 0), stop=(dk == 7)),
                                reads=[CUR["hk"][j], sk], writes=[bk])
                        P.op("act", lambda e, bank=bank, j=j, nb=nb: e.activation(
                            out=vn[:, j, nb * 512:(nb + 1) * 512], in_=bank[:, :],
                            func=AF.Gelu_apprx_tanh),
                            reads=[bk], writes=[("vn", j, nb)])
                        P.op("dve", lambda e, j=j, nb=nb: e.bn_stats(
                            out=stats[:, j, nb * 6:(nb + 1) * 6], in_=vn[:, j, nb * 512:(nb + 1) * 512]),
                            reads=[("vn", j, nb)], writes=[("stats", j, nb)])
                    issue_loads(st_used[0] + 3)
                for j in range(4):
                    P.op("dve", lambda e, j=j: e.bn_aggr(out=mv[:, j, :], in_=stats[:, j, :]),
                         reads=[("stats", j, nb) for nb in range(4)], writes=[("mv", j)])
                    P.op("dve", lambda e, j=j: e.tensor_scalar(
                        out=lve[:, j:j + 1], in0=mv[:, j, 1:2], scalar1=LN_EPS, scalar2=None,
                        op0=ALU.add), reads=[("mv", j)], writes=[("lve", j)])
                    P.op("pool", lambda e, j=j: e.tensor_tensor(
                        out=lrstd[:, j:j + 1], in0=lve[:, j:j + 1], in1=neg_half[:, :], op=ALU.pow),
                        reads=[("lve", j), "neg_half"], writes=[("lrstd", j)])
                    P.op("dve", lambda e, j=j: e.scalar_tensor_tensor(
                        out=nmr[:, j:j + 1], in0=mv[:, j, 0:1], scalar=-1.0, in1=lrstd[:, j:j + 1],
                        op0=ALU.mult, op1=ALU.mult),
                        reads=[("mv", j), ("lrstd", j)], writes=[("nmr", j)])
                    P.op("act", lambda e, j=j: e.activation(
                        out=vn[:, j, :], in_=vn[:, j, :], func=AF.Identity,
                        scale=lrstd[:, j:j + 1], bias=nmr[:, j:j + 1]),
                        reads=[("vn", j, nb) for nb in range(4)] + [("lrstd", j), ("nmr", j)],
                        writes=[("vn", j, nb) for nb in range(4)])
                chk("p1")
                for cg in range(4):
                    stU, skU = next_stage()
                    stG, skG = next_stage()
                    for ci in range(4):
                        cc = cg * 4 + ci
                        g = cc // 2
                        bU, bkU = PS.next()
                        for dk in range(8):
                            P.op("pe", lambda e, bU=bU, stU=stU, dk=dk, ci=ci, hT=CUR["hT"]: e.matmul(
                                bU[:, :], lhsT=stU[:, dk, ci * 128:(ci + 1) * 128], rhs=hT[:, dk, :],
                                start=(dk == 0), stop=(dk == 7)),
                                reads=CUR["hk"] + [skU], writes=[bkU])
                        bG, bkG = PS.next()
                        for dk in range(8):
                            P.op("pe", lambda e, bG=bG, stG=stG, dk=dk, ci=ci, hT=CUR["hT"]: e.matmul(
                                bG[:, :], lhsT=stG[:, dk, ci * 128:(ci + 1) * 128], rhs=hT[:, dk, :],
                                start=(dk == 0), stop=(dk == 7)),
                                reads=CUR["hk"] + [skG], writes=[bkG])
                        bS, bkS = PS.next()
                        for j in range(4):
                            P.op("pe", lambda e, bS=bS, j=j, cc=cc, g=g: e.matmul(
                                bS[:, j * 128:(j + 1) * 128], lhsT=vn[:, j, cc * 128:(cc + 1) * 128],
                                rhs=wcT[:, g, :], start=True, stop=True),
                                reads=[("vn", j, cc // 4), "wcT"], writes=[bkS])
                        GU, gk = GUr.next()
                        TH, tk = THr.next()
                        SSt, sk2 = SSr.next()
                        P.op("act", lambda e, GU=GU, bU=bU: e.activation(
                            out=GU[:, :], in_=bU[:, :], func=AF.Gelu_apprx_tanh),
                            reads=[bkU], writes=[gk])
                        P.op("act", lambda e, TH=TH, bG=bG: e.activation(
                            out=TH[:, :], in_=bG[:, :], func=AF.Tanh, scale=0.5),
                            reads=[bkG], writes=[tk])
                        P.op("dve", lambda e, SSt=SSt, bS=bS, cc=cc: e.scalar_tensor_tensor(
                            out=SSt[:, :].rearrange("p (j t) -> p j t", j=4),
                            in0=bS[:, :].rearrange("p (j t) -> p j t", j=4),
                            scalar=lnw_pp[:, cc:cc + 1],
                            in1=biasT[:, cc, :].unsqueeze(1).broadcast_to([128, 4, 128]),
                            op0=ALU.mult, op1=ALU.add),
                            reads=[bkS, "lnw", "biasT"], writes=[sk2])
                        P.op("dve", lambda e, TH=TH, bG=bG: e.scalar_tensor_tensor(
                            out=TH[:, :], in0=TH[:, :], scalar=1.0, in1=bG[:, :],
                            op0=ALU.add, op1=ALU.mult),
                            reads=[tk, bkG], writes=[tk])
                        P.op("dve", lambda e, GU=GU, TH=TH: e.scalar_tensor_tensor(
                            out=GU[:, :], in0=GU[:, :], scalar=0.5, in1=TH[:, :],
                            op0=ALU.mult, op1=ALU.mult),
                            reads=[gk, tk], writes=[gk])
                        P.op("dve", lambda e, GU=GU, SSt=SSt, cc=cc: e.tensor_tensor(
                            out=yT[:, cc, :], in0=SSt[:, :], in1=GU[:, :], op=ALU.mult),
                            reads=[gk, sk2], writes=[("yT", cc)])
                    issue_loads(st_used[0] + 3)
                chk("p2")
                if tb == NTB - 1 and doB:
                    a_early = (["lnw", "biasT", "wcT"] + [("stg", i) for i in range(4)]
                               + [("stats", j, nb) for j in range(4) for nb in range(4)]
                               + [(k, j) for k in ("mv", "lve", "lrstd", "nmr") for j in range(4)]
                               + [(r, k) for r in ("GU", "TH", "SS") for k in range(2)]
                               + [("vn", j, nb) for j in range(4) for nb in range(4)])
                    P.op("pool", lambda e: e.memset(ones_col[:, :], 1.0), writes=a_early + B_EARLY_KEYS)
                for j in range(4):
                    jt = tb * 4 + j
                    for dh in range(2):
                        bank, bk = PS.next()
                        for cc in range(16):
                            P.op("pe", lambda e, bank=bank, cc=cc, j=j, dh=dh: e.matmul(
                                bank[:, :], lhsT=yT[:, cc, j * 128:(j + 1) * 128],
                                rhs=woutA[:, cc, dh * 512:(dh + 1) * 512],
                                start=(cc == 0), stop=(cc == 15)),
                                reads=[("yT", cc), ("woutA", cc // 4)], writes=[bk])
                        residual_add(bank, bk, jt, dh)
                chk("p3")

        if doB:
            if doA:
                P.fence(lambda e: e.memset(neg_half[:, :], -0.5))
            cur[0] = REG0
            cw_pp, _, _ = alloc("cw_pp", [128, 48], F32)
            cb_pp, _, _ = alloc("cb_pp", [128, 12], F32)
            ba_pp, _, _ = alloc("ba_pp", [128, 12], F32)
            bx_pp, _, _ = alloc("bx_pp", [128, 12], F32)
            k1_pp, _, _ = alloc("k1_pp", [128, 12], F32)
            k2_pp, _, _ = alloc("k2_pp", [128, 12], F32)
            halo, _, _ = alloc("halo", [128, 12, 4], F32)
            hstate, _, _ = alloc("hstate", [128, 12], F32)
            quarter, _, _ = alloc("quarter", [128, 1], F32)
            if final_norm:
                NFW[0] = alloc("nfw_b", [128, D], F32)[0]
            NE = 6
            winB0, _, _ = alloc("winB0", [128, 8, 2 * NE * 128], BF16)
            waB, _, _ = alloc("waB", [128, 12, 128], BF16)
            wxB, _, _ = alloc("wxB", [128, 12, 128], BF16)
            NSET = int(os.environ.get("KNSET", "3"))
            HDr = Ring("HD", [alloc("HD%d" % i, [128, 4], F32)[0] for i in range(3)])
            XCr = Ring("XC", [alloc("XC%d" % i, [128, 512], F32)[0] for i in range(NSET)])
            XCBr = Ring("XCB", [alloc("XCB%d" % i, [128, 512], BF16)[0] for i in range(2)])
            THRr = Ring("THR", [alloc("THR%d" % i, [128, 512], F32)[0] for i in range(NSET)])
            Ar = Ring("A", [alloc("A%d" % i, [128, 512], F32)[0] for i in range(NSET)])
            THIr = Ring("THI", [alloc("THI%d" % i, [128, 512], F32)[0] for i in range(NSET)])
            THGr = Ring("THG", [alloc("THG%d" % i, [128, 512], F32)[0] for i in range(NSET)])
            if doA:
                assert cur[0] <= A_EARLY_END, (cur[0], A_EARLY_END)
            winB1, _, _ = alloc("winB1", [128, 8, 2 * (12 - NE) * 128], BF16)
            woutB, _, _ = alloc("woutB", [128, 12, D], BF16)
            yTB, _, _ = alloc("yTB", [128, 12, TB], BF16)

            def wsel(part, c):
                if c < NE:
                    return winB0, part * NE * 128 + c * 128
                return winB1, part * (12 - NE) * 128 + (c - NE) * 128

            P.skip_fence = True
            if final_norm:
                P.dma("sp", lambda e: e.dma_start(out=NFW[0][:, :],
                                                  in_=norm_f_w_d.partition_broadcast(128)),
                      "nfw", writes=["nfw"])

            if os.environ.get("KMEM"):
                print("B mem end", cur[0], "top", top, "spare", top - cur[0], "REG0", REG0, "base", base)
            def pp_load(dst, src_ap, key):
                P.dma("sp", ncdma(lambda e: e.dma_start(out=dst, in_=src_ap)), key, writes=[key])
            pp_load(cw_pp[:, :].rearrange("p (k c) -> p k c", k=4),
                    b_conv_w_d.rearrange("k (c p) -> p k c", p=128), "cw")
            pp_load(cb_pp[:, :], b_conv_b_d.rearrange("(c p) -> p c", p=128), "cb")
            pp_load(ba_pp[:, :], b_ga_b_d.rearrange("(c p) -> p c", p=128), "ba")
            pp_load(bx_pp[:, :], b_gx_b_d.rearrange("(c p) -> p c", p=128), "bx")
            pp_load(k1_pp[:, :], b_lam_d.rearrange("(c p) -> p c", p=128), "k1")
            WSC = 3

            def winB_dma(part, lo, hi, hfkey, grp):
                wt, d0 = wsel(part, lo)
                c0 = part * BW + lo * 128
                w = (hi - lo) * 128
                P.dma("pool", lambda e: e.dma_start(
                    out=wt[:, :, d0:d0 + w],
                    in_=b_w_in_d[:, c0:c0 + w].rearrange("(k p) n -> p k n", p=128)),
                    ("winB", part, hfkey), reads=[("bch", grp - 1, i) for i in range(2)],
                    writes=[("winB", part, hfkey), ("bch", grp, part)])
            winB_dma(0, 0, WSC, 0, 0)
            winB_dma(1, 0, WSC, 0, 0)
            P.dma("pool", lambda e: e.dma_start(out=waB[:, :, :],
                                                in_=b_ga_w_d.rearrange("h i j -> i h j")),
                  "waB", reads=[("bch", 0, i) for i in range(2)], writes=["waB", ("bch", 1, 0)])
            P.dma("pool", lambda e: e.dma_start(out=wxB[:, :, :],
                                                in_=b_gx_w_d.rearrange("h i j -> i h j")),
                  "wxB", reads=[("bch", 0, i) for i in range(2)], writes=["wxB", ("bch", 1, 1)])
            winB_dma(0, WSC, NE, 1, 2)
            winB_dma(1, WSC, NE, 1, 2)
            P.skip_fence = False
            winB_dma(0, NE, 12, 2, 3)
            winB_dma(1, NE, 12, 2, 3)
            for q2 in range(2):
                P.dma("pool", lambda e, q2=q2: e.dma_start(
                    out=woutB[:, q2 * 6:(q2 + 1) * 6, :],
                    in_=b_w_out_d[q2 * 768:(q2 + 1) * 768, :].rearrange("(c p) d -> p c d", p=128)),
                    ("woutB", q2), reads=[("bch", 3, i) for i in range(2)], writes=[("woutB", q2)])
            P.skip_fence = True
            P.op("pool", lambda e: e.memset(halo[:, :, :], 0.0), writes=[("halo", c) for c in range(12)])
            P.op("pool", lambda e: e.memset(hstate[:, :], 0.0), writes=[("hstate", c) for c in range(12)])
            P.op("pool", lambda e: e.memset(quarter[:, :], 0.25), writes=["quarter"])
            P.op("act", lambda e: e.activation(out=k1_pp[:, :], in_=k1_pp[:, :], func=AF.Exp, scale=-1.0),
                 reads=["k1"], writes=["k1"])
            P.op("act", lambda e: e.activation(out=k1_pp[:, :], in_=k1_pp[:, :], func=AF.Ln, bias=1.0),
                 reads=["k1"], writes=["k1"])
            P.op("dve", lambda e: e.tensor_scalar(out=k2_pp[:, :], in0=k1_pp[:, :], scalar1=-8.0,
                                                  scalar2=None, op0=ALU.mult),
                 reads=["k1"], writes=["k2"])
            P.op("dve", lambda e: e.tensor_scalar(out=k1_pp[:, :], in0=k1_pp[:, :], scalar1=-4.0,
                                                  scalar2=None, op0=ALU.mult),
                 reads=["k1"], writes=["k1"])
            P.op("dve", lambda e: e.tensor_scalar(out=ba_pp[:, :], in0=ba_pp[:, :], scalar1=0.5,
                                                  scalar2=None, op0=ALU.mult),
                 reads=["ba"], writes=["ba"])
            P.op("dve", lambda e: e.tensor_scalar(out=bx_pp[:, :], in0=bx_pp[:, :], scalar1=0.5,
                                                  scalar2=None, op0=ALU.mult),
                 reads=["bx"], writes=["bx"])
            P.skip_fence = False

            def st1(T):
                c, hf = T["c"], T["hf"]
                for nm, ring in (("XC", XCr), ("XCB", XCBr), ("THR", THRr), ("A", Ar),
                                 ("THI", THIr), ("THG", THGr)):
                    T[nm], T["k" + nm] = ring.next()
                bX, bkX = PSB["chk"].next()
                T["bX"], T["bkX"] = bX, bkX
                wtx, dx0 = wsel(0, c)
                wtg, dg0 = wsel(1, c)
                for dk in range(8):
                    P.op("pe", lambda e, bX=bX, dk=dk, hT=CUR["hT"], wtx=wtx, dx0=dx0: e.matmul(
                        bX[:, :], lhsT=wtx[:, dk, dx0:dx0 + 128], rhs=hT[:, dk, :],
                        start=(dk == 0), stop=(dk == 7)),
                        reads=CUR["hk"] + [("winB", 0, hf)], writes=[bkX])
                bG, bkG = PSB["chk"].next()
                T["bG"], T["bkG"] = bG, bkG
                for dk in range(8):
                    P.op("pe", lambda e, bG=bG, dk=dk, hT=CUR["hT"], wtg=wtg, dg0=dg0: e.matmul(
                        bG[:, :], lhsT=wtg[:, dk, dg0:dg0 + 128],
                        rhs=hT[:, dk, :], start=(dk == 0), stop=(dk == 7)),
                        reads=CUR["hk"] + [("winB", 1, hf)], writes=[bkG])
                P.op("act", lambda e, XC=T["XC"], bX=bX, c=c: e.activation(
                    out=XC[:, :], in_=bX[:, :], func=AF.Identity,
                    scale=cw_pp[:, 36 + c:37 + c], bias=cb_pp[:, c:c + 1]),
                    reads=[bkX, "cw", "cb"], writes=[T["kXC"]])
                for k in (2, 1, 0):
                    sh = 3 - k
                    P.op("dve", lambda e, XC=T["XC"], bX=bX, k=k, sh=sh, c=c: e.scalar_tensor_tensor(
                        out=XC[:, sh:512], in0=bX[:, 0:512 - sh],
                        scalar=cw_pp[:, k * 12 + c:k * 12 + c + 1], in1=XC[:, sh:512],
                        op0=ALU.mult, op1=ALU.add),
                        reads=[bkX, T["kXC"], "cw"], writes=[T["kXC"]])
                    if os.environ.get("KHEAD", "pool") == "pool":
                        hd, khd = HDr.next()
                        P.op("pool", lambda e, hd=hd, k=k, sh=sh, c=c: e.tensor_scalar(
                            out=hd[:, 0:sh], in0=halo[:, c, 4 - sh:4],
                            scalar1=cw_pp[:, k * 12 + c:k * 12 + c + 1], scalar2=None, op0=ALU.mult),
                            reads=[("halo", c), "cw"], writes=[khd])
                        P.op("pool", lambda e, hd=hd, XC=T["XC"], sh=sh: e.tensor_tensor(
                            out=XC[:, 0:sh], in0=XC[:, 0:sh], in1=hd[:, 0:sh], op=ALU.add),
                            reads=[khd, T["kXC"]], writes=[T["kXC"]])
                    else:
                        P.op("dve", lambda e, XC=T["XC"], k=k, sh=sh, c=c: e.scalar_tensor_tensor(
                            out=XC[:, 0:sh], in0=halo[:, c, 4 - sh:4],
                            scalar=cw_pp[:, k * 12 + c:k * 12 + c + 1], in1=XC[:, 0:sh],
                            op0=ALU.mult, op1=ALU.add),
                            reads=[("halo", c), T["kXC"], "cw"], writes=[T["kXC"]])
                if os.environ.get("KHALO", "dve") == "act":
                    P.op("act", lambda e, bX=bX, c=c: e.activation(
                        out=halo[:, c, :], in_=bX[:, 508:512], func=AF.Copy),
                        reads=[bkX], writes=[("halo", c)])
                else:
                    P.op("dve", lambda e, bX=bX, c=c: e.tensor_copy(out=halo[:, c, :], in_=bX[:, 508:512]),
                         reads=[bkX], writes=[("halo", c)])
                if os.environ.get("KCAST", "act") == "pool":
                    P.op("pool", lambda e, XCB=T["XCB"], XC=T["XC"]: e.tensor_copy(out=XCB[:, :], in_=XC[:, :]),
                         reads=[T["kXC"]], writes=[T["kXCB"]])
                else:
                    P.op("act", lambda e, XCB=T["XCB"], XC=T["XC"]: e.activation(
                        out=XCB[:, :], in_=XC[:, :], func=AF.Copy),
                        reads=[T["kXC"]], writes=[T["kXCB"]])
                bR, bkR = PSB["chk"].next()
                T["bR"], T["bkR"] = bR, bkR
                P.op("pe", lambda e, bR=bR, XCB=T["XCB"], c=c: e.matmul(
                    bR[:, :], lhsT=waB[:, c, :], rhs=XCB[:, :], start=True, stop=True),
                    reads=["waB", T["kXCB"]], writes=[bkR])
                bI, bkI = PSB["chk"].next()
                T["bI"], T["bkI"] = bI, bkI
                P.op("pe", lambda e, bI=bI, XCB=T["XCB"], c=c: e.matmul(
                    bI[:, :], lhsT=wxB[:, c, :], rhs=XCB[:, :], start=True, stop=True),
                    reads=["wxB", T["kXCB"]], writes=[bkI])
                P.op("act", lambda e, THG=T["THG"], bG=bG: e.activation(
                    out=THG[:, :], in_=bG[:, :], func=AF.Tanh, scale=0.5),
                    reads=[bkG], writes=[T["kTHG"]])
                P.op("dve", lambda e, THG=T["THG"], bG=bG: e.scalar_tensor_tensor(
                    out=THG[:, :], in0=THG[:, :], scalar=1.0, in1=bG[:, :],
                    op0=ALU.add, op1=ALU.mult),
                    reads=[T["kTHG"], bkG], writes=[T["kTHG"]])

            def st2(T):
                c = T["c"]
                P.op("act", lambda e, THR=T["THR"], bR=T["bR"], c=c: e.activation(
                    out=THR[:, :], in_=bR[:, :], func=AF.Tanh, scale=0.5, bias=ba_pp[:, c:c + 1]),
                    reads=[T["bkR"], "ba"], writes=[T["kTHR"]])
                P.op("act", lambda e, THI=T["THI"], bI=T["bI"], c=c: e.activation(
                    out=THI[:, :], in_=bI[:, :], func=AF.Tanh, scale=0.5, bias=bx_pp[:, c:c + 1]),
                    reads=[T["bkI"], "bx"], writes=[T["kTHI"]])
                P.op("act", lambda e, At=T["A"], THR=T["THR"], c=c: e.activation(
                    out=At[:, :], in_=THR[:, :], func=AF.Exp,
                    scale=k1_pp[:, c:c + 1], bias=k1_pp[:, c:c + 1]),
                    reads=[T["kTHR"], "k1"], writes=[T["kA"]])
                e2eng = os.environ.get("KE2", "dve")
                if e2eng == "act":
                    P.op("act", lambda e, THR=T["THR"], c=c: e.activation(
                        out=THR[:, :], in_=THR[:, :], func=AF.Exp,
                        scale=k2_pp[:, c:c + 1], bias=k2_pp[:, c:c + 1]),
                        reads=[T["kTHR"], T["kA"], "k2"], writes=[T["kTHR"]])
                else:
                    P.op(e2eng, lambda e, THR=T["THR"], At=T["A"]: e.tensor_tensor(
                        out=THR[:, :], in0=At[:, :], in1=At[:, :], op=ALU.mult),
                        reads=[T["kA"]], writes=[T["kTHR"]])
                P.op("dve", lambda e, THI=T["THI"], XC=T["XC"]: e.scalar_tensor_tensor(
                    out=THI[:, :], in0=THI[:, :], scalar=1.0, in1=XC[:, :],
                    op0=ALU.add, op1=ALU.mult),
                    reads=[T["kTHI"], T["kXC"]], writes=[T["kTHI"]])

            def st3(T):
                c = T["c"]
                P.op("act", lambda e, Et=T["THR"]: e.activation(
                    out=Et[:, :], in_=Et[:, :], func=AF.Sqrt, scale=-0.25, bias=quarter[:, 0:1]),
                    reads=[T["kTHR"], "quarter"], writes=[T["kTHR"]])
                P.op(os.environ.get("KBENG", "dve"), lambda e, THI=T["THI"], Et=T["THR"]: e.tensor_tensor(
                    out=THI[:, :], in0=THI[:, :], in1=Et[:, :], op=ALU.mult),
                    reads=[T["kTHI"], T["kTHR"]], writes=[T["kTHI"]])
                P.op("dve", lambda e, Ht=T["XC"], At=T["A"], THI=T["THI"], c=c: e.tensor_tensor_scan(
                    out=Ht[:, :], data0=At[:, :], data1=THI[:, :], initial=hstate[:, c:c + 1],
                    op0=ALU.mult, op1=ALU.add),
                    reads=[T["kA"], T["kTHI"], ("hstate", c)], writes=[T["kXC"]])
                P.op("pool", lambda e, Ht=T["XC"], c=c: e.tensor_copy(out=hstate[:, c:c + 1], in_=Ht[:, 511:512]),
                     reads=[T["kXC"]], writes=[("hstate", c)])
                P.op("dve", lambda e, THG=T["THG"], Ht=T["XC"], c=c: e.scalar_tensor_tensor(
                    out=yTB[:, c, :], in0=Ht[:, :], scalar=0.5, in1=THG[:, :],
                    op0=ALU.mult, op1=ALU.mult),
                    reads=[T["kXC"], T["kTHG"]], writes=[("yTB", c)])


            def phase3(tb, c_lo=0, c_hi=12, last=True):
                for j in range(4):
                    jt = tb * 4 + j
                    for dh in range(2):
                        bank, bk = PSB["blk"].next()
                        for c in range(c_lo, c_hi):
                            P.op("pe", lambda e, bank=bank, c=c, j=j, dh=dh: e.matmul(
                                bank[:, :], lhsT=yTB[:, c, j * 128:(j + 1) * 128],
                                rhs=woutB[:, c, dh * 512:(dh + 1) * 512],
                                start=(c == c_lo), stop=(c == c_hi - 1)),
                                reads=[("yTB", c), ("woutB", c // 6)], writes=[bk])
                        residual_add(bank, bk, jt, dh)
                if final_norm and last:
                    for j in range(4):
                        finalize(tb * 4 + j)

            def start_block(tb):
                P.skip_fence = (tb == 0)
                phase0(1, tb)
                P.skip_fence = False

            if os.environ.get("KXTB", "1") == "1":
                PSB["blk"] = PSP
                PSB["chk"] = PSC
            allc = [dict(c=c, hf=(0 if c < WSC else (1 if c < NE else 2)), tb=tb)
                    for tb in range(NTB) for c in range(12)]
            NEARLY = int(os.environ.get("KNEARLY", "3")) if doA else 0
            G = len(allc)
            if os.environ.get("KXTB", "1") == "1":
                for g in range(G + 2):
                    if g < G:
                        if allc[g]["c"] == 0:
                            start_block(allc[g]["tb"])
                        P.skip_fence = g < NEARLY
                        st1(allc[g])
                        P.skip_fence = False
                    if 0 <= g - 1 < G:
                        P.skip_fence = (g - 1) < NEARLY
                        st2(allc[g - 1])
                        P.skip_fence = False
                    if 0 <= g - 2 < G:
                        st3(allc[g - 2])
                        tb3, c3 = allc[g - 2]["tb"], allc[g - 2]["c"]
                        SPL = int(os.environ.get("KSPLIT", "9"))
                        if SPL and tb3 == NTB - 1:
                            if c3 == SPL - 1:
                                phase3(tb3, 0, SPL, last=False)
                            elif c3 == 11:
                                phase3(tb3, SPL, 12, last=True)
                        elif c3 == 11:
                            phase3(tb3)
            else:
                for tb in range(NTB):
                    start_block(tb)
                    chunks = allc[tb * 12:(tb + 1) * 12]
                    for i in range(12 + 2):
                        if i < 12:
                            st1(chunks[i])
                        if 0 <= i - 1 < 12:
                            st2(chunks[i - 1])
                        if 0 <= i - 2 < 12:
                            st3(chunks[i - 2])
                    phase3(tb)
    except _StopBuild:
        pass

    if final_norm and NFW[0] is None:
        NFW[0] = nc.alloc_sbuf_tensor_at("nfw_b", [128, D], F32, offset=hT_off)
        P.dma("sp", lambda e: e.dma_start(out=NFW[0][:, :], in_=norm_f_w_d.partition_broadcast(128)),
              "nfw", writes=[("hT", 0, j) for j in range(4)] + ["nfw"])
    for jt in range(NT):
        if jt not in done_tiles:
            finalize(jt)
    P.emit()
    _EST[0] = P.est_us
    _EST.append(P)
    return nc


_CACHE = {}


def _get(layers, final_norm):
    key = (layers, final_norm)
    if key not in _CACHE:
        _CACHE[key] = build_program(layers, final_norm)
    return _CACHE[key]


A_NAMES = ["a_w_in", "a_ln_w", "a_ln_b", "a_w_s", "a_b_s", "a_w_out"]
B_NAMES = ["b_w_in", "b_conv_w", "b_conv_b", "b_gate_a_w", "b_gate_a_b", "b_gate_x_w",
           "b_gate_x_b", "b_lambda", "b_w_out"]


def _run(layers, final_norm, x, w):
    nc = _get(layers, final_norm)
    names = ["norm_w", "norm_f_w"]
    if "A" in layers:
        names += A_NAMES
    if "B" in layers:
        names += B_NAMES
    in_maps = []
    for b in range(8):
        m = {"x": np.ascontiguousarray(x[b])}
        for n in names:
            m[n] = w[n]
        in_maps.append(m)
    res = run_bass_kernel_spmd(nc, in_maps, core_ids=list(range(8)))
    return np.stack([r["out"] for r in res.results], axis=0)


FUSED = True


def kernel(**inputs):
    w = {}
    for k, v in inputs.items():
        if k == "x":
            continue
        a = np.asarray(v, dtype=np.float32)
        if k not in ("norm_w", "norm_f_w"):
            a = a[0]
        w[k] = np.ascontiguousarray(a)
    x = np.asarray(inputs["x"], dtype=np.float32)
    if FUSED:
        return _run("AB", True, x, w)
    x1 = _run("A", False, x, w)
    return _run("B", True, x1, w)
```
